# Optimizing a Trainium2 kernel written in Bass

```python
import math
import jax, jax.numpy as jnp
from jax import lax
import numpy as np

D_MODEL = 1024
BATCH = 16
SEQ = 4096
DEPTH = 2

CTX_LEN = 256
GRID_W = 64
EPS = 1e-6

A_WIDTH = D_MODEL // 4
A_HEADS = 4
A_HEAD_DIM = A_WIDTH // A_HEADS
CHUNK = 128

B_WIDTH = D_MODEL // 2
B_HEADS = 4
B_V_DIM = B_WIDTH // B_HEADS
B_QK_DIM = B_V_DIM // 2
ROPE_AXIS_DIM = B_QK_DIM // 2
ROPE_THETA = 10000.0
Q_BLOCK = 128

C_WIDTH = D_MODEL // 4
C_CONV = 31

MIX_WIDTH = A_WIDTH + B_WIDTH + C_WIDTH
QK_COLS = B_HEADS * 2 * B_QK_DIM
COL_A = 0
COL_Q = COL_A + 2 * A_WIDTH
COL_K = COL_Q + QK_COLS
COL_V = COL_K + QK_COLS
COL_C = COL_V + B_WIDTH
IN_COLS = COL_C + 2 * C_WIDTH

D_FF = ((8 * D_MODEL // 3 + 127) // 128) * 128
FFN_CONV = 3

kernel_name = "hybrid_diff_gmlp_conformer_dit"

F32 = jnp.float32


def rms_norm(x, g):
    xf = x.astype(F32)
    y = xf * lax.rsqrt(jnp.mean(xf * xf, axis=-1, keepdims=True) + EPS)
    return (y * g.astype(F32)).astype(x.dtype)


def layer_norm(x, g, b):
    xf = x.astype(F32)
    xc = xf - jnp.mean(xf, axis=-1, keepdims=True)
    y = xc * lax.rsqrt(jnp.mean(xc * xc, axis=-1, keepdims=True) + EPS)
    return (y * g.astype(F32) + b.astype(F32)).astype(x.dtype)


def dwconv(x, w, b):
    k = w.shape[0]
    y = lax.conv_general_dilated(
        x, w[:, None, :].astype(x.dtype), window_strides=(1,), padding=[(k // 2, k // 2)],
        dimension_numbers=("NWC", "WIO", "NWC"), feature_group_count=x.shape[-1])
    return y + b


def adaln(cvec, w, b, n):
    m = jax.nn.silu(cvec) @ w[:, : n * D_MODEL] + b[: n * D_MODEL]
    return jnp.split(m, n, axis=-1)


def modulate(x, shift, scale):
    return x * (1 + scale) + shift


def axial_rope_tables(n_tokens, dtype):
    rows = n_tokens // GRID_W
    row = jnp.repeat(jnp.arange(rows), GRID_W).astype(F32)
    col = jnp.tile(jnp.arange(GRID_W), rows).astype(F32)
    inv = ROPE_THETA ** (-jnp.arange(0, ROPE_AXIS_DIM, 2, dtype=F32) / ROPE_AXIS_DIM)
    ang = jnp.stack([row[:, None] * inv, col[:, None] * inv], axis=1)
    ang = ang[None, :, None, None]
    return (jnp.cos(ang).astype(dtype), jnp.sin(ang).astype(dtype))


def axial_rope(t, cos, sin):
    sh = t.shape
    t = t.reshape(*sh[:-1], 2, 2, ROPE_AXIS_DIM // 2)
    t1, t2 = t[..., 0, :], t[..., 1, :]
    out = jnp.stack([t1 * cos - t2 * sin, t2 * cos + t1 * sin], axis=-2)
    return out.reshape(sh)


def qk_heads(p, g, rope):
    b, n, _ = p.shape
    t = rms_norm(p.reshape(b, n, B_HEADS, 2, B_QK_DIM), g)
    return t if rope is None else axial_rope(t, *rope)


def project(h, P, rope):
    p = h @ P["w_in"]
    b, n, _ = p.shape
    q = qk_heads(p[..., COL_Q:COL_K], P["q_norm_g"], rope)
    k = qk_heads(p[..., COL_K:COL_V], P["k_norm_g"], rope)
    v = p[..., COL_V:COL_C].reshape(b, n, B_HEADS, B_V_DIM)
    return (p[..., COL_A:COL_Q], q, k, v, p[..., COL_C:])


def diff_attn(q, k, v, lam):
    s = jnp.einsum("bqhmd,bkhmd->bhmqk", q.astype(F32), k.astype(F32)) * (B_QK_DIM ** -0.5)
    p = jax.nn.softmax(s, axis=-1)
    a = p[:, :, 0] - lam * p[:, :, 1]
    return jnp.einsum("bhqk,bkhd->bqhd", a, v.astype(F32))


def blocked_diff_attn(q, k, v, lam):
    b, n = q.shape[:2]
    qb = jnp.moveaxis(q.reshape(b, n // Q_BLOCK, Q_BLOCK, *q.shape[2:]), 1, 0)
    o = lax.map(lambda qi: diff_attn(qi, k, v, lam), qb)
    return jnp.moveaxis(o, 0, 1).reshape(b, n, B_HEADS, B_V_DIM)


def chunk_gmlp(pa, P):
    z = jax.nn.gelu(pa)
    u, v = jnp.split(z, 2, axis=-1)
    v = layer_norm(v, P["ln_v_g"], P["ln_v_b"])
    b, n, _ = v.shape
    vb = v.reshape(b, n // CHUNK, CHUNK, A_HEADS, A_HEAD_DIM)
    gate = jnp.einsum("hpq,bnqhd->bnphd", P["w_s"], vb) + P["b_s"].T[:, :, None]
    return u * gate.reshape(b, n, A_WIDTH)


def conformer_conv(pc, P):
    a, g = jnp.split(pc, 2, axis=-1)
    y = dwconv(a * jax.nn.sigmoid(g), P["conv_w"], P["conv_b"])
    return jax.nn.silu(layer_norm(y, P["ln_c_g"], P["ln_c_b"]))


def mixer_merge(pa, o, pc, P, lam_init):
    b, n, _ = pa.shape
    ya = chunk_gmlp(pa, P)
    yb = (rms_norm(o, P["subln_g"]) * (1.0 - lam_init)).astype(pa.dtype).reshape(b, n, B_WIDTH)
    yc = conformer_conv(pc, P)
    return jnp.concatenate([ya, yb, yc], axis=-1) @ P["w_out"]


def conv_ffn(h, P):
    gt = dwconv(h @ P["w_gate"], P["ffn_conv_w"], P["ffn_conv_b"])
    return (jax.nn.silu(gt) * (h @ P["w_val"])) @ P["w_down"]


def setup_inputs(seed: int = 0) -> dict:
    key = jax.random.key(seed)
    ks = jax.random.split(key, 30)
    L = DEPTH

    def nrm(i, shape, s):
        return jax.random.normal(ks[i], shape, F32) * s

    return {
        "x": nrm(0, (BATCH, SEQ, D_MODEL), 1.0),
        "c": nrm(1, (BATCH, D_MODEL), 1.0),
        "ctx": nrm(2, (BATCH, CTX_LEN, D_MODEL), 1.0),
        "c_ctx": nrm(3, (D_MODEL,), 1.0),
        "w_mod": nrm(4, (L, D_MODEL, 6 * D_MODEL), D_MODEL ** -0.5),
        "b_mod": nrm(5, (L, 6 * D_MODEL), 0.02),
        "norm1_g": 1.0 + nrm(6, (L, D_MODEL), 0.05),
        "w_in": nrm(7, (L, D_MODEL, IN_COLS), D_MODEL ** -0.5),
        "ln_v_g": 1.0 + nrm(8, (L, A_WIDTH), 0.05),
        "ln_v_b": nrm(9, (L, A_WIDTH), 0.02),
        "w_s": nrm(10, (L, A_HEADS, CHUNK, CHUNK), CHUNK ** -0.5),
        "b_s": nrm(11, (L, A_HEADS, CHUNK), 0.02),
        "q_norm_g": 1.0 + nrm(12, (L, B_QK_DIM), 0.05),
        "k_norm_g": 1.0 + nrm(13, (L, B_QK_DIM), 0.05),
        "lam_q1": nrm(14, (L, B_QK_DIM), 0.1),
        "lam_k1": nrm(15, (L, B_QK_DIM), 0.1),
        "lam_q2": nrm(16, (L, B_QK_DIM), 0.1),
        "lam_k2": nrm(17, (L, B_QK_DIM), 0.1),
        "subln_g": 1.0 + nrm(18, (L, B_V_DIM), 0.05),
        "conv_w": nrm(19, (L, C_CONV, C_WIDTH), C_CONV ** -0.5),
        "conv_b": nrm(20, (L, C_WIDTH), 0.02),
        "ln_c_g": 1.0 + nrm(21, (L, C_WIDTH), 0.05),
        "ln_c_b": nrm(22, (L, C_WIDTH), 0.02),
        "w_out": nrm(23, (L, MIX_WIDTH, D_MODEL), MIX_WIDTH ** -0.5),
        "norm2_g": 1.0 + nrm(24, (L, D_MODEL), 0.05),
        "w_gate": nrm(25, (L, D_MODEL, D_FF), D_MODEL ** -0.5),
        "w_val": nrm(26, (L, D_MODEL, D_FF), D_MODEL ** -0.5),
        "ffn_conv_w": nrm(27, (L, FFN_CONV, D_FF), FFN_CONV ** -0.5),
        "ffn_conv_b": nrm(28, (L, D_FF), 0.02),
        "w_down": nrm(29, (L, D_FF, D_MODEL), D_FF ** -0.5),
    }


def reference(x, c, ctx, c_ctx, w_mod, b_mod, norm1_g, w_in, ln_v_g, ln_v_b, w_s, b_s,
              q_norm_g, k_norm_g, lam_q1, lam_k1, lam_q2, lam_k2, subln_g, conv_w, conv_b,
              ln_c_g, ln_c_b, w_out, norm2_g, w_gate, w_val, ffn_conv_w, ffn_conv_b, w_down):
    rope = axial_rope_tables(x.shape[1], x.dtype)
    x_lat, x_ctx = x, ctx
    for l in range(DEPTH):
        last = l == DEPTH - 1
        P = {
            "w_in": w_in[l], "ln_v_g": ln_v_g[l], "ln_v_b": ln_v_b[l], "w_s": w_s[l], "b_s": b_s[l],
            "q_norm_g": q_norm_g[l], "k_norm_g": k_norm_g[l], "subln_g": subln_g[l],
            "conv_w": conv_w[l], "conv_b": conv_b[l], "ln_c_g": ln_c_g[l], "ln_c_b": ln_c_b[l],
            "w_out": w_out[l], "w_gate": w_gate[l], "w_val": w_val[l],
            "ffn_conv_w": ffn_conv_w[l], "ffn_conv_b": ffn_conv_b[l], "w_down": w_down[l],
        }
        lam_init = 0.8 - 0.6 * math.exp(-0.3 * l)
        lam = (jnp.exp(jnp.sum(lam_q1[l].astype(F32) * lam_k1[l].astype(F32)))
               - jnp.exp(jnp.sum(lam_q2[l].astype(F32) * lam_k2[l].astype(F32))) + lam_init)

        cmod = adaln(c_ctx, w_mod[l], b_mod[l], 2 if last else 6)
        hc = modulate(rms_norm(x_ctx, norm1_g[l]), cmod[0], cmod[1])
        if last:
            pkv = hc @ w_in[l][:, COL_K:COL_C]
            k_c = qk_heads(pkv[..., :QK_COLS], k_norm_g[l], None)
            v_c = pkv[..., QK_COLS:].reshape(hc.shape[0], hc.shape[1], B_HEADS, B_V_DIM)
        else:
            pa_c, q_c, k_c, v_c, pc_c = project(hc, P, None)
            o_c = diff_attn(q_c, k_c, v_c, lam)
            x_ctx = x_ctx + cmod[2] * mixer_merge(pa_c, o_c, pc_c, P, lam_init)
            h2c = modulate(rms_norm(x_ctx, norm2_g[l]), cmod[3], cmod[4])
            x_ctx = x_ctx + cmod[5] * conv_ffn(h2c, P)

        sh1, sc1, g1, sh2, sc2, g2 = [m[:, None, :] for m in adaln(c, w_mod[l], b_mod[l], 6)]
        hl = modulate(rms_norm(x_lat, norm1_g[l]), sh1, sc1)
        pa, q, k, v, pc = project(hl, P, rope)
        k_all = jnp.concatenate([k, k_c], axis=1)
        v_all = jnp.concatenate([v, v_c], axis=1)
        o = blocked_diff_attn(q, k_all, v_all, lam)
        x_lat = x_lat + g1 * mixer_merge(pa, o, pc, P, lam_init)
        h2 = modulate(rms_norm(x_lat, norm2_g[l]), sh2, sc2)
        x_lat = x_lat + g2 * conv_ffn(h2, P)
    return x_lat
```

```python
import contextlib
import math
import numpy as np
import concourse.bass as bass
import concourse.mybir as mybir
from concourse.bass_utils import run_bass_kernel_spmd

F32 = mybir.dt.float32
BF16 = mybir.dt.bfloat16
AF = mybir.ActivationFunctionType
ALU = mybir.AluOpType
AX = mybir.AxisListType

D = 1024
KC = 8
GRID_W = 64
EPS = 1e-6
IN_COLS = 2560
COL_Q, COL_K, COL_V, COL_C = 512, 1024, 1536, 2048
DFF = 2816
NJ = 22
CONVK = 31
N_CORES = 8

ENGS = ("pe", "act", "dve", "pool", "sp")
EPOCH = 20000
N_DMA_SEMS = 24
ROPE_ENG = "pool"


class Op:
    __slots__ = ("eng", "fn", "deps", "signal", "sig", "dma", "idx", "prev_dma")

    def __init__(self, eng, fn, dma, idx):
        self.eng = eng
        self.fn = fn
        self.dma = dma
        self.deps = []
        self.signal = dma
        self.sig = None
        self.idx = idx
        self.prev_dma = None


class Sched:
    def __init__(self):
        self.ops = []
        self.lastw = {}
        self.readers = {}
        self.bar = None
        self.last_eng = {}
        self.dma_rr = {e: 0 for e in ENGS}
        self.dma_cnt = {}
        self.dma_last = {}

    def _link(self, op, d):
        if d is op:
            return
        if d.dma:
            op.deps.append(d)
        elif d.eng != op.eng:
            d.signal = True
            op.deps.append(d)
        elif op.dma or op.eng != "pe":
            d.signal = True
            op.deps.append(d)

    def add(self, eng, fn, r=(), w=(), dma=False):
        op = Op(eng, fn, dma, len(self.ops))
        deps = {}
        if self.bar is not None:
            deps[self.bar.idx] = self.bar
        for k in r:
            d = self.lastw.get(k)
            if d is not None:
                deps[d.idx] = d
        for k in w:
            d = self.lastw.get(k)
            if d is not None:
                deps[d.idx] = d
            for rd in self.readers.get(k, ()):
                deps[rd.idx] = rd
        for d in deps.values():
            self._link(op, d)
        for k in r:
            self.readers.setdefault(k, []).append(op)
        for k in w:
            self.lastw[k] = op
            self.readers[k] = []
        if dma:
            k = self.dma_rr[eng] % N_DMA_SEMS
            self.dma_rr[eng] += 1
            key = ("dma", eng, k)
            self.dma_cnt[key] = self.dma_cnt.get(key, 0) + 16
            op.sig = (key, self.dma_cnt[key])
            op.prev_dma = self.dma_last.get(key)
            self.dma_last[key] = op
        else:
            self.last_eng[eng] = op
        self.ops.append(op)
        return op

    def barrier(self, fn):
        op = Op("dve", fn, False, len(self.ops))
        for e, d in self.last_eng.items():
            self._link(op, d)
        for d in self.dma_last.values():
            op.deps.append(d)
        self.last_eng["dve"] = op
        self.ops.append(op)
        self.bar = op
        self.lastw = {}
        self.readers = {}
        return op

    def emit(self, nc, stack):
        cnt = {e: 0 for e in ENGS}
        n_epochs = {e: 1 for e in ENGS}
        for op in self.ops:
            if (not op.dma) and op.signal:
                cnt[op.eng] += 1
                ep = (cnt[op.eng] - 1) // EPOCH
                n_epochs[op.eng] = ep + 1
                op.sig = (("c", op.eng, ep), cnt[op.eng] - ep * EPOCH)
        sems = {}

        def get_sem(key):
            if key not in sems:
                sems[key] = stack.enter_context(nc.semaphore("s_" + "_".join(str(x) for x in key)))
            return sems[key]

        for e in ENGS:
            for ep in range(n_epochs[e]):
                get_sem(("c", e, ep))
        for key in self.dma_cnt:
            get_sem(key)
        per_eng = {e: [op for op in self.ops if op.eng == e] for e in ENGS}
        dma_last = self.dma_last

        def run_engine(ename, eng):
            waited = {}
            for op in per_eng[ename]:
                need = {}
                for d in op.deps:
                    key, val = d.sig
                    if waited.get(key, 0) < val and need.get(key, 0) < val:
                        need[key] = val
                if op.dma and op.prev_dma is not None:
                    key, val = op.prev_dma.sig
                    if waited.get(key, 0) < val and need.get(key, 0) < val:
                        need[key] = val
                for key, val in need.items():
                    eng.wait_ge(get_sem(key), val)
                    waited[key] = val
                inst = op.fn(eng)
                if op.dma:
                    inst.then_inc(get_sem(op.sig[0]), 16)
                elif op.signal:
                    inst.then_inc(get_sem(op.sig[0]), 1)
            for key, op in dma_last.items():
                if key[1] == ename:
                    k, val = op.sig
                    if waited.get(k, 0) < val:
                        eng.wait_ge(get_sem(k), val)

        block = stack.enter_context(nc.Block())

        @block.tensor
        def _(e):
            run_engine("pe", e)

        @block.scalar
        def _(e):
            run_engine("act", e)

        @block.vector
        def _(e):
            run_engine("dve", e)

        @block.gpsimd
        def _(e):
            run_engine("pool", e)

        @block.sync
        def _(e):
            run_engine("sp", e)


class Arena:
    def __init__(self, t, nwords):
        self.t = t
        self.n = nwords
        self.off = 0

    def alloc(self, free_shape, dt):
        esz = 2 if dt == BF16 else 4
        nel = 1
        for s in free_shape:
            nel *= s
        nw = (nel * esz + 3) // 4
        nw = (nw + 7) // 8 * 8
        assert self.off + nw <= self.n, ("arena overflow", self.off, nw, self.n)
        v = self.t[:, self.off:self.off + nw]
        self.off += nw
        if dt == BF16:
            v = v.bitcast(BF16)
        v = v[:, 0:nel]
        if len(free_shape) == 2:
            v = v.rearrange("p (a b) -> p a b", a=free_shape[0])
        elif len(free_shape) == 3:
            v = v.rearrange("p (a b c) -> p a b c", a=free_shape[0], b=free_shape[1])
        return v


def build_program(L, C, NB, DEPTH, arena_kb=160):
    T = L + C
    NT = T // 128
    NTL = L // 128
    nc = bass.Bass("TRN2", target_bir_lowering=False)
    S = Sched()

    def din(name, shape, dt=F32):
        return nc.dram_tensor(name, list(shape), dt, kind="ExternalInput").ap()

    x_in = din("x", [NB, L, D])
    ctx_in = din("ctx", [NB, C, D])
    cT_d = din("cT", [128, KC, NB + 1])
    w_mod_d = din("w_mod", [DEPTH, D, 6 * D])
    b_modT_d = din("b_modT", [DEPTH, 128, 48])
    n1g_d = din("norm1_gT", [DEPTH, 128, KC])
    n2g_d = din("norm2_gT", [DEPTH, 128, KC])
    w_in_d = din("w_in", [DEPTH, D, IN_COLS])
    lnv_d = din("ln_v", [DEPTH, 2, 256])
    w_sT_d = din("w_sT", [DEPTH, 4, 128, 128])
    b_sT_d = din("b_sT", [DEPTH, 128, 4])
    qkg_d = din("qk_g", [DEPTH, 2, 64])
    lam_d = din("lam", [DEPTH, 4, 64])
    subg_d = din("subln_gT", [DEPTH, 128, 1])
    convw_d = din("conv_wT", [DEPTH, 128, 2, CONVK])
    cvec_d = din("c_vecT", [DEPTH, 128, 3, 2])
    w_out_d = din("w_out", [DEPTH, D, D])
    w_gate_d = din("w_gate", [DEPTH, D, DFF])
    w_val_d = din("w_val", [DEPTH, D, DFF])
    fcw_d = din("ffn_conv_wT", [DEPTH, 128, NJ, 3])
    fcb_d = din("ffn_conv_bT", [DEPTH, 128, NJ])
    w_down_d = din("w_down", [DEPTH, DFF, D])
    rcos_d = din("rope_cos", [128, NTL, 64])
    rsin_d = din("rope_sin", [128, NTL, 64])
    out_d = nc.dram_tensor("out", [NB, L, D], F32, kind="ExternalOutput").ap()

    xs_d = nc.dram_tensor("xs", [NB, T, D], F32).ap()
    ycat_d = nc.dram_tensor("ycat", [NB, 8, 128, T], BF16).ap()
    h2T_d = nc.dram_tensor("h2T", [NB, 8, 128, T], BF16).ap()
    uc_d = nc.dram_tensor("uc", [NB, 2, 128, T], BF16).ap()
    qt_d = nc.dram_tensor("qt", [NB, 4, 128, T], BF16).ap()
    vt_d = nc.dram_tensor("vt", [NB, NT, 128, 512], BF16).ap()

    st = contextlib.ExitStack()
    with st:
        def sb(name, shape, dt=F32):
            return st.enter_context(nc.sbuf_tensor(name, list(shape), dt))

        identf = sb("identf", [128, 128])
        identb = sb("identb", [128, 128], BF16)
        iot = sb("iot", [128, 128])
        onesb = sb("onesb", [128, 128], BF16)
        onesf = sb("onesf", [128, 128])
        ones256 = sb("ones256", [128, 128])
        ones128 = sb("ones128", [128, 128])
        barsc = sb("barsc", [128, 1])
        rcos = sb("rcos", [128, NTL, 64])
        rsin = sb("rsin", [128, NTL, 64])
        cTs = sb("cTs", [128, KC, NB + 1])
        siluT = sb("siluT", [128, KC, NB + 1])
        modT = sb("modT", [128, 48, NB + 1])
        bmod = sb("bmod", [128, 48])
        n1g = sb("n1g", [128, KC])
        n2g = sb("n2g", [128, KC])
        A1 = sb("A1", [128, KC, NB + 1])
        A2 = sb("A2", [128, KC, NB + 1])
        lnvg = sb("lnvg", [128, 256])
        lnvb = sb("lnvb", [128, 256])
        bsT = sb("bsT", [128, 4])
        gq = sb("gq", [128, 64])
        gk = sb("gk", [128, 64])
        lamv = sb("lamv", [128, 4, 64])
        lamp = sb("lamp", [128, 2, 64])
        lams = sb("lams", [128, 2])
        neglam = sb("neglam", [128, 1])
        subg = sb("subg", [128, 1])
        convw = sb("convw", [128, 2, CONVK])
        cvec = sb("cvec", [128, 3, 2])
        fcw = sb("fcw", [128, NJ, 3])
        fcb = sb("fcb", [128, NJ])
        wsT = sb("wsT", [128, 4, 128], BF16)
        wsTf = sb("wsTf", [128, 4, 128])
        arena_t = sb("arena", [128, arena_kb * 256])
        pp = st.enter_context(nc.psum_tensor("pp", [128, 8, 512], F32))

        def bank(i):
            return pp[:, i, :]

        def bankb(i):
            return pp[:, i, :].bitcast(BF16)

        def bk(i):
            return ("bank", i)

        def mm(out, lhsT, rhs, start, stop, r, w):
            S.add("pe", lambda e: e.matmul(out, lhsT=lhsT, rhs=rhs, start=start, stop=stop), r, w)

        def tr(out, in_, r, w):
            S.add("pe", lambda e: e.transpose(out=out, in_=in_, identity=identb[:]), r, w)

        def act(out, in_, func, r, w, scale=1.0, bias=0.0, accum=None):
            if accum is None:
                S.add("act", lambda e: e.activation(out=out, in_=in_, func=func, bias=bias, scale=scale), r, w)
            else:
                S.add("act", lambda e: e.activation(out=out, in_=in_, func=func, bias=bias, scale=scale,
                                                    accum_out=accum), r, w)

        def tt(eng, out, in0, in1, op, r, w):
            S.add(eng, lambda e: e.tensor_tensor(out=out, in0=in0, in1=in1, op=op), r, w)

        def ts(eng, out, in0, s1, op0, r, w, s2=None, op1=None):
            if op1 is None:
                S.add(eng, lambda e: e.tensor_scalar(out=out, in0=in0, scalar1=s1, scalar2=None, op0=op0), r, w)
            else:
                S.add(eng, lambda e: e.tensor_scalar(out=out, in0=in0, scalar1=s1, scalar2=s2, op0=op0, op1=op1), r, w)

        def stt(eng, out, in0, scalar, in1, op0, op1, r, w):
            S.add(eng, lambda e: e.scalar_tensor_tensor(out=out, in0=in0, scalar=scalar, in1=in1, op0=op0, op1=op1), r, w)

        def cp(eng, out, in_, r, w):
            if eng == "act":
                S.add("act", lambda e: e.copy(out=out, in_=in_), r, w)
            else:
                S.add(eng, lambda e: e.tensor_copy(out=out, in_=in_), r, w)

        def memset(eng, ap, val, w):
            S.add(eng, lambda e: e.memset(ap, val), (), w)

        def dma(eng, out, in_, r=(), w=()):
            S.dma_op = S.add(eng, lambda e: e.dma_start(out=out, in_=in_), r, w, dma=True)

        def barrier():
            S.barrier(lambda e: e.memset(barsc[:], 0.0))

        def rsqrt_act(out, in_, r, w, scale=1.0, eps=EPS):
            act(out, in_, AF.Ln, r, w, scale=scale, bias=epsb[:, 0:1] if eps == EPS else eps)
            act(out, out, AF.Exp, w, w, scale=-0.5)

        epsb = sb("epsb", [128, 1])

        memset("dve", epsb[:], EPS, ["epsb"])
        S.add("pool", lambda e: e.iota(iot[:], pattern=[[1, 128]], base=0, channel_multiplier=-1,
                                       allow_small_or_imprecise_dtypes=True), (), ["iot"])
        S.add("dve", lambda e: e.tensor_single_scalar(out=identf[:], in_=iot[:], scalar=0.0, op=ALU.is_equal),
              ["iot"], ["identf"])
        cp("dve", identb[:], identf[:], ["identf"], ["identb"])
        memset("dve", onesb[:], 1.0, ["onesb"])
        memset("dve", onesf[:], 1.0, ["onesf"])
        memset("dve", ones256[:], 1.0 / 256, ["ones256"])
        memset("dve", ones128[:], 1.0 / 128, ["ones128"])
        dma("sp", rcos[:], rcos_d, w=["rcos"])
        dma("sp", rsin[:], rsin_d, w=["rsin"])
        dma("sp", cTs[:], cT_d, w=["cTs"])
        act(siluT[:], cTs[:], AF.Silu, ["cTs"], ["siluT"])
        barrier()

        def seg_tiles(b):
            res = []
            for s0 in range(0, L, 512):
                res.append((s0, min(512, L - s0), False, 0, L))
            for s0 in range(0, C, 512):
                res.append((L + s0, min(512, C - s0), True, L, C))
            return res

        def x_src(l, b, tok0, n):
            if l == 0:
                if tok0 < L:
                    return x_in[b, tok0:tok0 + n, :]
                return ctx_in[b, tok0 - L:tok0 - L + n, :]
            return xs_d[b, tok0:tok0 + n, :]

        for l in range(DEPTH):
            last = (l == DEPTH - 1)
            lam_init = 0.8 - 0.6 * math.exp(-0.3 * l)
            dma("sp", bmod[:], b_modT_d[l], w=["bmod"])
            dma("sp", n1g[:], n1g_d[l], w=["n1g"])
            dma("sp", n2g[:], n2g_d[l], w=["n2g"])
            dma("sp", lnvg[:], lnv_d[l, 0:1, :].broadcast_to([128, 256]), w=["lnvg"])
            dma("sp", lnvb[:], lnv_d[l, 1:2, :].broadcast_to([128, 256]), w=["lnvb"])
            dma("sp", bsT[:], b_sT_d[l], w=["bsT"])
            dma("sp", gq[:], qkg_d[l, 0:1, :].broadcast_to([128, 64]), w=["gq"])
            dma("sp", gk[:], qkg_d[l, 1:2, :].broadcast_to([128, 64]), w=["gk"])
            for i in range(4):
                dma("sp", lamv[:, i, :], lam_d[l, i:i + 1, :].broadcast_to([128, 64]), w=[("lamv", i)])
            dma("sp", subg[:], subg_d[l], w=["subg"])
            dma("sp", convw[:], convw_d[l], w=["convw"])
            dma("sp", cvec[:], cvec_d[l], w=["cvec"])
            dma("sp", fcw[:], fcw_d[l], w=["fcw"])
            dma("sp", fcb[:], fcb_d[l], w=["fcb"])
            dma("sp", wsTf[:], w_sT_d[l].rearrange("h q p -> q h p"), w=["wsTf"])
            cp("dve", wsT[:], wsTf[:], ["wsTf"], ["wsT"])
            ts("dve", gq[:], gq[:], 0.125, ALU.mult, ["gq"], ["gq"])
            tt("dve", lamp[:, 0, :], lamv[:, 0, :], lamv[:, 1, :], ALU.mult, [("lamv", 0), ("lamv", 1)], ["lamp0"])
            tt("dve", lamp[:, 1, :], lamv[:, 2, :], lamv[:, 3, :], ALU.mult, [("lamv", 2), ("lamv", 3)], ["lamp1"])
            S.add("dve", lambda e: e.tensor_reduce(out=lams[:], in_=lamp[:], axis=AX.X, op=ALU.add),
                  ["lamp0", "lamp1"], ["lams"])
            act(lams[:], lams[:], AF.Exp, ["lams"], ["lams"])
            tt("dve", neglam[:], lams[:, 1:2], lams[:, 0:1], ALU.subtract, ["lams"], ["neglam"])
            ts("dve", neglam[:], neglam[:], -lam_init, ALU.add, ["neglam"], ["neglam"])
            ts("dve", subg[:], subg[:], 1.0 - lam_init, ALU.mult, ["subg"], ["subg"])

            ar = Arena(arena_t, arena_kb * 256)
            NBLK = 8
            BW = 6 * D // NBLK
            wm = [ar.alloc([KC, BW], F32) for _ in range(2)]
            for blk in range(NBLK):
                wt = wm[blk % 2]
                for k in range(KC):
                    dma("sp", wt[:, k, :], w_mod_d[l, k * 128:(k + 1) * 128, blk * BW:(blk + 1) * BW],
                        w=[("wm", blk % 2, k)])
                pbi = blk % 2
                nj = BW // 128
                for jj in range(nj):
                    for k in range(KC):
                        mm(bank(pbi)[:, jj * 4:jj * 4 + NB + 1], wt[:, k, jj * 128:(jj + 1) * 128], siluT[:, k, :],
                           k == 0, k == KC - 1, [("wm", blk % 2, k), "siluT"], [bk(pbi)])
                j0 = blk * nj
                tt("dve", modT[:, j0:j0 + nj, :],
                   bank(pbi)[:, 0:nj * 4].rearrange("p (j c) -> p j c", c=4)[:, :, 0:NB + 1],
                   bmod[:, j0:j0 + nj].unsqueeze(2).broadcast_to([128, nj, NB + 1]), ALU.add,
                   [bk(pbi), "bmod"], ["modT"])
            stt("dve", A1[:], modT[:, 8:16, :], 1.0, n1g[:].unsqueeze(2).broadcast_to([128, KC, NB + 1]),
                ALU.add, ALU.mult, ["modT", "n1g"], ["A1"])
            stt("dve", A2[:], modT[:, 32:40, :], 1.0, n2g[:].unsqueeze(2).broadcast_to([128, KC, NB + 1]),
                ALU.add, ALU.mult, ["modT", "n2g"], ["A2"])
            barrier()

            def stream_of(b, is_ctx):
                return NB if is_ctx else b

            def norm_modT(xt, sq, ms, rstd, xn, hT_out, Acol, Bcol, pbank, kx, kms, kxn, khT):
                act(sq, xt, AF.Square, [kx], ["sq", kms], scale=1.0 / 32, accum=ms)
                rsqrt_act(rstd, ms, [kms], [kms])
                ts("dve", xn, xt, rstd, ALU.mult, [kx, kms], [kxn])
                pb = bankb(pbank).rearrange("p (k n) -> p k n", k=KC)
                for k in range(KC):
                    tr(pb[:, k, :], xn[:, k * 128:(k + 1) * 128], [kxn], [bk(pbank)])
                for k in range(KC):
                    act(hT_out[:, k, :], pb[:, k, :], AF.Identity, [bk(pbank), "A", "modT"], [khT],
                        scale=Acol[:, k:k + 1], bias=Bcol[:, k:k + 1])

            def gvec_rep(dst, jbase, s, pbank):
                for half in range(2):
                    for kk in range(4):
                        k = half * 4 + kk
                        dg = dgs[k % 2]
                        ts("dve", dg, identf[:], modT[:, jbase + k, s:s + 1], ALU.mult, [], [("dg", k % 2)])
                        mm(bank(pbank)[:, kk * 128:(kk + 1) * 128], onesf[:], dg, True, True,
                           [("dg", k % 2)], [bk(pbank)])
                    cp("dve", dst[:, half * 512:(half + 1) * 512], bank(pbank), [bk(pbank)], ["grep"])

            for b in range(NB):
                ar = Arena(arena_t, arena_kb * 256)
                KT = ar.alloc([4, T], BF16)
                w_in = ar.alloc([KC, IN_COLS], BF16)
                if b == 0:
                    for k in range(KC):
                        dma("pool", w_in[:, k, :], w_in_d[l, k * 128:(k + 1) * 128, :], w=[("w_in", k)])
                p1_mark = ar.off
                xts = [ar.alloc([D], F32) for _ in range(2)]
                sq = ar.alloc([D], F32)
                ms = ar.alloc([2], F32)
                xn = ar.alloc([D], BF16)
                hTs = [ar.alloc([KC, 512], BF16) for _ in range(2)]
                Vsts = [ar.alloc([4, 512], BF16) for _ in range(2)]
                sig = ar.alloc([512], F32)
                ucs = [ar.alloc([2, 512], BF16) for _ in range(2)]
                zts = [ar.alloc([512], F32) for _ in range(2)]
                bnst = ar.alloc([8], F32)
                vn = ar.alloc([256], F32)
                vnbs = [ar.alloc([256], BF16) for _ in range(2)]
                yas = [ar.alloc([256], BF16) for _ in range(2)]
                yaTs = [ar.alloc([2, 512], BF16) for _ in range(2)]
                qsqs = [ar.alloc([512], F32) for _ in range(2)]
                stt_ = ar.alloc([24], F32)
                qns = [ar.alloc([512], F32) for _ in range(2)]
                qcss = [ar.alloc([1024], F32) for _ in range(2)]
                qrbs = [[ar.alloc([512], BF16) for _ in range(2)] for _ in range(2)]
                QTs = [ar.alloc([4, 512], BF16) for _ in range(2)]
                wk = [("w_in", k) for k in range(KC)]
                tiles = seg_tiles(b)
                xi_box = [0]

                def emit_norm(si, i):
                    tok0, n, is_ctx, seg0, seglen = tiles[si]
                    s = stream_of(b, is_ctx)
                    xt = xts[xi_box[0] % 2]
                    kx = ("xt", xi_box[0] % 2)
                    xi_box[0] += 1
                    dma("sp", xt, x_src(l, b, tok0 + i * 128, 128), w=[kx])
                    norm_modT(xt, sq, ms[:, 0:1], ms[:, 1:2], xn, hTs[si % 2][:, :, i * 128:(i + 1) * 128],
                              A1[:, :, s], modT[:, 0:8, s], 0, kx, "ms", "xn", ("hT", si % 2))

                def emit_blocks(si, i):
                    tok0, n, is_ctx, seg0, seglen = tiles[si]
                    hT = hTs[si % 2]
                    khT = ("hT", si % 2)
                    ti = (tok0 + i * 128) // 128
                    par = i % 2
                    zt = zts[par]
                    vnb = vnbs[par]
                    hasq = not (last and is_ctx)
                    lhs = [hT[:, k, i * 128:(i + 1) * 128] for k in range(KC)]
                    chains = [(1, COL_K, gk, 1, 0)] + ([(0, COL_Q, gq, 6, 1)] if hasq else [])
                    nst = 2 + 8 * len(chains)
                    v3 = lambda a: a.rearrange("p (g d) -> p g d", g=8)
                    v5 = lambda a: a.rearrange("p (g a h f) -> p g a h f", g=8, a=2, h=2)
                    for k in range(KC):
                        mm(bank(3), lhs[k], w_in[:, k, 0:512], k == 0, k == KC - 1, [khT, wk[k]], [bk(3)])
                    for which, col, g, pbi, sc in chains:
                        for k in range(KC):
                            mm(bank(pbi), lhs[k], w_in[:, k, col:col + 512], k == 0, k == KC - 1,
                               [khT, wk[k]], [bk(pbi)])
                    for k in range(KC):
                        mm(bank(2), lhs[k], w_in[:, k, COL_V:COL_V + 512], k == 0, k == KC - 1,
                           [khT, wk[k]], [bk(2)])
                    act(zt, bank(3), AF.Gelu_apprx_tanh, [bk(3)], [("zt", par)])
                    for which, col, g, pbi, sc in chains:
                        act(qsqs[sc], bank(pbi), AF.Square, [bk(pbi)], [("qsq", sc)], scale=0.125)
                    cp("act", Vsts[si % 2][:, i, :], bank(2), [bk(2)], [("Vst", si % 2)])
                    S.add("dve", lambda e, bnst=bnst, zt=zt: e.bn_stats(out=bnst[:, 0:6], in_=zt[:, 256:512]),
                          [("zt", par)], ["bnst"])
                    S.add("dve", lambda e, bnst=bnst, stt_=stt_: e.bn_aggr(out=stt_[:, 0:2], in_=bnst[:, 0:6]),
                          ["bnst"], ["st"])
                    for which, col, g, pbi, sc in chains:
                        S.add("dve", lambda e, sc=sc, stt_=stt_: e.tensor_reduce(
                            out=stt_[:, 2 + 8 * sc:10 + 8 * sc], in_=qsqs[sc].rearrange("p (g d) -> p g d", g=8),
                            axis=AX.X, op=ALU.add), [("qsq", sc)], ["st"])
                    rsqrt_act(stt_[:, 1:nst], stt_[:, 1:nst], ["st"], ["st"])
                    ts("dve", vn, zt[:, 256:512], stt_[:, 0:1], ALU.subtract, [("zt", par), "st"], ["vn"],
                       s2=stt_[:, 1:2], op1=ALU.mult)
                    tt("dve", vn, vn, lnvg[:], ALU.mult, ["vn", "lnvg"], ["vn"])
                    tt("dve", vnb, vn, lnvb[:], ALU.add, ["vn", "lnvb"], [("vnb", par)])
                    for which, col, g, pbi, sc in chains:
                        qn_ = qns[sc]
                        qc = qcss[sc][:, 0:512]
                        qs_ = qcss[sc][:, 512:1024]
                        qrb = qrbs[which][par]
                        kqrb = ("qrb", which, par)
                        tt("dve", v3(qn_), v3(bank(pbi)),
                           stt_[:, 2 + 8 * sc:10 + 8 * sc].unsqueeze(2).broadcast_to([128, 8, 64]), ALU.mult,
                           [bk(pbi), "st"], [("qn", sc)])
                        gb = g[:].unsqueeze(1).broadcast_to([128, 8, 64])
                        if is_ctx:
                            tt("dve", v3(qrb), v3(qn_), gb, ALU.mult, [("qn", sc), "g"], [kqrb])
                        else:
                            tt("dve", v3(qn_), v3(qn_), gb, ALU.mult, [("qn", sc), "g"], [("qn", sc)])
                            cb = rcos[:, ti, :].unsqueeze(1).broadcast_to([128, 8, 64])
                            sbb = rsin[:, ti, :].unsqueeze(1).broadcast_to([128, 8, 64])
                            tt(ROPE_ENG, v3(qc), v3(qn_), cb, ALU.mult, [("qn", sc), "rcos"], [("qcs", sc)])
                            tt(ROPE_ENG, v3(qs_), v3(qn_), sbb, ALU.mult, [("qn", sc), "rsin"], [("qcs", sc)])
                            for ax in range(2):
                                tt(ROPE_ENG, v5(qrb)[:, :, ax, 0, :], v5(qc)[:, :, ax, 0, :], v5(qs_)[:, :, ax, 1, :],
                                   ALU.subtract, [("qcs", sc)], [kqrb])
                                tt(ROPE_ENG, v5(qrb)[:, :, ax, 1, :], v5(qc)[:, :, ax, 1, :], v5(qs_)[:, :, ax, 0, :],
                                   ALU.add, [("qcs", sc)], [kqrb])

                def emit_dep(si, i):
                    tok0, n, is_ctx, seg0, seglen = tiles[si]
                    ti = (tok0 + i * 128) // 128
                    par = i % 2
                    zt = zts[par]
                    vnb = vnbs[par]
                    ya = yas[par]
                    yaT = yaTs[si % 2]
                    for h in range(4):
                        mm(bank(4)[:, h * 64:(h + 1) * 64], wsT[:, h, :], vnb[:, h * 64:(h + 1) * 64],
                           True, True, [("vnb", par), "wsT"], [bk(4)])
                    for h in range(4):
                        stt("dve", ya[:, h * 64:(h + 1) * 64], bank(4)[:, h * 64:(h + 1) * 64], bsT[:, h:h + 1],
                            zt[:, h * 64:(h + 1) * 64], ALU.add, ALU.mult, [bk(4), ("zt", par), "bsT"], [("ya", par)])
                    pbT = bankb(5).rearrange("p (k n) -> p k n", k=8)
                    for cj in range(2):
                        tr(pbT[:, cj, :], ya[:, cj * 128:(cj + 1) * 128], [("ya", par)], [bk(5)])
                    cp("dve", yaT[:, :, i * 128:(i + 1) * 128], pbT[:, 0:2, :], [bk(5)], [("yaT", si % 2)])
                    for which in range(2):
                        if which == 0 and last and is_ctx:
                            continue
                        qrb = qrbs[which][par]
                        pbT = bankb(7).rearrange("p (k n) -> p k n", k=8)
                        for h in range(4):
                            tr(pbT[:, h, :], qrb[:, h * 128:(h + 1) * 128], [("qrb", which, par)], [bk(7)])
                        if which == 0:
                            cp("act", QTs[si % 2][:, :, i * 128:(i + 1) * 128], pbT[:, 0:4, :], [bk(7)], [("QTs", si % 2)])
                        else:
                            cp("act", KT[:, :, ti * 128:(ti + 1) * 128], pbT[:, 0:4, :], [bk(7)], [("KT", ti)])

                for i in range(tiles[0][1] // 128):
                    emit_norm(0, i)
                for si, (tok0, n, is_ctx, seg0, seglen) in enumerate(tiles):
                    hT = hTs[si % 2]
                    khT = ("hT", si % 2)
                    nt = n // 128
                    nt_next = tiles[si + 1][1] // 128 if si + 1 < len(tiles) else 0
                    uc = ucs[si % 2]
                    kuc = ("ucs", si % 2)
                    if not (last and is_ctx):
                        for cj in range(2):
                            for part, pbi in ((0, 1), (1, 2)):
                                col = COL_C + part * 256 + cj * 128
                                for k in range(KC):
                                    mm(bank(pbi)[:, 0:n], w_in[:, k, col:col + 128], hT[:, k, 0:n],
                                       k == 0, k == KC - 1, [khT, wk[k]], [bk(pbi)])
                            act(sig[:, 0:n], bank(2)[:, 0:n], AF.Sigmoid, [bk(2)], ["sig"])
                            tt("dve", uc[:, cj, 0:n], bank(1)[:, 0:n], sig[:, 0:n], ALU.mult, [bk(1), "sig"], [kuc])
                        dma("pool", uc_d[b, :, :, tok0:tok0 + n].rearrange("c p t -> p c t"), uc[:, :, 0:n], r=[kuc])
                    for i in range(nt):
                        emit_blocks(si, i)
                        if i > 0:
                            emit_dep(si, i - 1)
                        if i < nt_next:
                            emit_norm(si + 1, i)
                    emit_dep(si, nt - 1)
                    for i in range(nt, nt_next):
                        emit_norm(si + 1, i)
                    dma("pool", ycat_d[b, 0:2, :, tok0:tok0 + n].rearrange("c p t -> p c t"),
                        yaTs[si % 2][:, :, 0:n], r=[("yaT", si % 2)])
                    if not (last and is_ctx):
                        dma("pool", qt_d[b, :, :, tok0:tok0 + n].rearrange("c p t -> p c t"),
                            QTs[si % 2][:, :, 0:n], r=[("QTs", si % 2)])
                    dma("pool", vt_d[b, tok0 // 128:tok0 // 128 + nt, :, :].rearrange("t p c -> p t c"),
                        Vsts[si % 2][:, 0:nt, :], r=[("Vst", si % 2)])
                barrier()

                ar.off = p1_mark
                dg = ar.alloc([2 * CONVK, 128], BF16)
                for cj in range(2):
                    for k in range(CONVK):
                        ts("dve", dg[:, cj * CONVK + k, :], identf[:], convw[:, cj, k:k + 1], ALU.mult, [], ["dgc"])
                HAL = CONVK // 2
                ucin = [ar.alloc([2, 512 + 2 * HAL], BF16) for _ in range(2)]
                yc = [ar.alloc([512], F32) for _ in range(2)]
                ysq = ar.alloc([512], F32)
                m2 = ar.alloc([512], F32)
                rstdc = ar.alloc([512], F32)
                tmpc = ar.alloc([512], F32)
                ycb = [ar.alloc([2, 512], BF16) for _ in range(2)]
                for si, (tok0, n, is_ctx, seg0, seglen) in enumerate(seg_tiles(b)):
                    if last and is_ctx:
                        continue
                    ui = ucin[si % 2]
                    kui = ("ucin", si % 2)
                    lo = max(seg0, tok0 - HAL)
                    hi = min(seg0 + seglen, tok0 + n + HAL)
                    c0 = lo - (tok0 - HAL)
                    wkeys = [kui]
                    if c0 > 0:
                        memset("dve", ui[:, :, 0:c0], 0.0, [kui])
                    if hi < tok0 + n + HAL:
                        memset("dve", ui[:, :, c0 + hi - lo:n + 2 * HAL], 0.0, [kui])
                    dma("sp", ui[:, :, c0:c0 + hi - lo], uc_d[b, :, :, lo:hi].rearrange("c p t -> p c t"), w=[kui])
                    for cj in range(2):
                        for k in range(CONVK):
                            mm(bank(cj)[:, 0:n], dg[:, cj * CONVK + k, :], ui[:, cj, k:k + n], k == 0, k == CONVK - 1,
                               [kui, "dgc"], [bk(cj)])
                        act(yc[cj][:, 0:n], bank(cj)[:, 0:n], AF.Identity, [bk(cj), "cvec"], [("yc", cj)],
                            bias=cvec[:, 0, cj:cj + 1])
                    for cj in range(2):
                        mm(bank(2)[:, 0:n], ones256[:], yc[cj][:, 0:n], cj == 0, cj == 1, [("yc", cj)], [bk(2)])
                    for cj in range(2):
                        act(ysq[:, 0:n], yc[cj][:, 0:n], AF.Square, [("yc", cj)], ["ysq"])
                        mm(bank(3)[:, 0:n], ones256[:], ysq[:, 0:n], cj == 0, cj == 1, ["ysq"], [bk(3)])
                    act(m2[:, 0:n], bank(2)[:, 0:n], AF.Square, [bk(2)], ["m2"])
                    tt("dve", rstdc[:, 0:n], bank(3)[:, 0:n], m2[:, 0:n], ALU.subtract, [bk(3), "m2"], ["rstdc"])
                    rsqrt_act(rstdc[:, 0:n], rstdc[:, 0:n], ["rstdc"], ["rstdc"])
                    yo = ycb[si % 2]
                    kyo = ("ycb", si % 2)
                    for cj in range(2):
                        tt("dve", tmpc[:, 0:n], yc[cj][:, 0:n], bank(2)[:, 0:n], ALU.subtract, [("yc", cj), bk(2)], ["tmpc"])
                        tt("dve", tmpc[:, 0:n], tmpc[:, 0:n], rstdc[:, 0:n], ALU.mult, ["tmpc", "rstdc"], ["tmpc"])
                        act(yo[:, cj, 0:n], tmpc[:, 0:n], AF.Silu, ["tmpc", "cvec"], [kyo],
                            scale=cvec[:, 1, cj:cj + 1], bias=cvec[:, 2, cj:cj + 1])
                    dma("pool", ycat_d[b, 6:8, :, tok0:tok0 + n].rearrange("c p t -> p c t"), yo[:, :, 0:n], r=[kyo])
                barrier()

                ar.off = p1_mark
                Vt = ar.alloc([NT, 512], BF16)
                for t0 in range(0, NT, 8):
                    t1 = min(NT, t0 + 8)
                    dma("sp", Vt[:, t0:t1, :], vt_d[b, t0:t1, :, :].rearrange("t p c -> p t c"), w=[("Vt", t0)])
                qin = [ar.alloc([512], BF16) for _ in range(2)]
                Et = [ar.alloc([2, 512], BF16) for _ in range(3)]
                rinv = ar.alloc([2, 512], F32)
                accs = [ar.alloc([2, 512], F32) for _ in range(2)]
                o1 = ar.alloc([512], F32)
                o2 = ar.alloc([512], F32)
                osq = ar.alloc([512], F32)
                rstda = ar.alloc([512], F32)
                ybo = [ar.alloc([512], BF16) for _ in range(2)]
                ui_ = 0
                ei = 0
                for si, (tok0, n, is_ctx, seg0, seglen) in enumerate(seg_tiles(b)):
                    if last and is_ctx:
                        continue
                    ktiles = list(range(NTL, NT)) if is_ctx else list(range(NT))
                    for h in range(4):
                        qi = qin[ui_ % 2]
                        kqi = ("qin", ui_ % 2)
                        yo = ybo[ui_ % 2]
                        kyo = ("ybo", ui_ % 2)
                        sb0 = 4 * 0
                        ui_ += 1
                        dma("sp", qi[:, 0:n], qt_d[b, h, :, tok0:tok0 + n], w=[kqi])
                        items = []
                        for kti, kt in enumerate(ktiles):
                            items.append((kti, kt, (ei % 2) * 2, Et[ei % 3], ("E", ei % 3)))
                            ei += 1

                        def emit_S(item, qi=qi, kqi=kqi, h=h, n=n):
                            kti, kt, sp_, E, kE = item
                            for m in range(2):
                                mm(bank(sp_ + m)[:, 0:n], KT[m * 64:(m + 1) * 64, h, kt * 128:(kt + 1) * 128],
                                   qi[m * 64:(m + 1) * 64, 0:n], True, True, [kqi], [bk(sp_), bk(sp_ + 1)])
                            act(E[:, :, 0:n], pp[:, sp_:sp_ + 2, 0:n], AF.Exp, [bk(sp_), bk(sp_ + 1)], [kE])

                        def emit_PV(item, h=h, n=n, nk=len(ktiles)):
                            kti, kt, sp_, E, kE = item
                            first = kti == 0
                            lastk = kti == nk - 1
                            for m in range(2):
                                mm(bank(4 + m)[:, 0:n], Vt[:, kt, h * 128:(h + 1) * 128], E[:, m, 0:n], first, lastk,
                                   [kE, ("Vt", (kt // 8) * 8)], [bk(4 + m)])
                            ac = accs[kti % 2]
                            for m, eng_ in ((0, "dve"), (1, "pool")):
                                ka = ("acc", kti % 2, m)
                                if kti < 2:
                                    cp(eng_, ac[:, m, 0:n], E[:, m, 0:n], [kE], [ka])
                                else:
                                    tt(eng_, ac[:, m, 0:n], ac[:, m, 0:n], E[:, m, 0:n], ALU.add, [kE, ka], [ka])

                        emit_S(items[0])
                        for ii in range(len(items)):
                            if ii + 1 < len(items):
                                emit_S(items[ii + 1])
                            emit_PV(items[ii])
                        sl = items[-1][2]
                        npar = min(2, len(items))
                        for m in range(2):
                            for pa in range(npar):
                                mm(bank(6 + m)[:, 0:n], onesf[:], accs[pa][:, m, 0:n], pa == 0, pa == npar - 1,
                                   [("acc", pa, m)], [bk(6 + m)])
                        S.add("dve", lambda e, n=n, rinv=rinv: e.reciprocal(out=rinv[:, :, 0:n], in_=pp[:, 6:8, 0:n]),
                              [bk(6), bk(7)], ["rinv"])
                        tt("dve", o1[:, 0:n], bank(4)[:, 0:n], rinv[:, 0, 0:n], ALU.mult, [bk(4), "rinv"], ["o1"])
                        tt("dve", o2[:, 0:n], bank(5)[:, 0:n], rinv[:, 1, 0:n], ALU.mult, [bk(5), "rinv"], ["o2"])
                        stt("dve", o1[:, 0:n], o2[:, 0:n], neglam[:, 0:1], o1[:, 0:n], ALU.mult, ALU.add,
                            ["o1", "o2", "neglam"], ["o1"])
                        act(osq[:, 0:n], o1[:, 0:n], AF.Square, ["o1"], ["osq"])
                        mm(bank(sl)[:, 0:n], ones128[:], osq[:, 0:n], True, True, ["osq"], [bk(sl)])
                        rsqrt_act(rstda[:, 0:n], bank(sl)[:, 0:n], [bk(sl)], ["rstda"])
                        tt("dve", o1[:, 0:n], o1[:, 0:n], rstda[:, 0:n], ALU.mult, ["o1", "rstda"], ["o1"])
                        ts("dve", yo[:, 0:n], o1[:, 0:n], subg[:, 0:1], ALU.mult, ["o1", "subg"], [kyo])
                        dma("pool", ycat_d[b, 2 + h, :, tok0:tok0 + n], yo[:, 0:n], r=[kyo])
                barrier()

            ar = Arena(arena_t, arena_kb * 256)
            w_out = ar.alloc([KC, D], BF16)
            for k in range(KC):
                dma("pool", w_out[:, k, :], w_out_d[l, k * 128:(k + 1) * 128, :], w=[("w_out", k)])
            dgs = [ar.alloc([128], F32) for _ in range(2)]
            g1rep = [ar.alloc([D], F32) for _ in range(2)]
            yct = [ar.alloc([8, 512], BF16) for _ in range(2)]
            xts = [ar.alloc([D], F32) for _ in range(3)]
            sq = ar.alloc([D], F32)
            ms = ar.alloc([2], F32)
            xn = ar.alloc([D], BF16)
            h2s = [ar.alloc([KC, 512], BF16) for _ in range(2)]
            for b in range(NB):
                gvec_rep(g1rep[0], 16, b, 0)
                if not last:
                    gvec_rep(g1rep[1], 16, NB, 0)
                xi = 0
                for si, (tok0, n, is_ctx, seg0, seglen) in enumerate(seg_tiles(b)):
                    if last and is_ctx:
                        continue
                    s = stream_of(b, is_ctx)
                    gr = g1rep[1 if is_ctx else 0]
                    yt = yct[si % 2]
                    kyt = ("yct", si % 2)
                    h2 = h2s[si % 2]
                    kh2 = ("h2s", si % 2)
                    dma("sp", yt[:, :, 0:n], ycat_d[b, :, :, tok0:tok0 + n].rearrange("c p t -> p c t"), w=[kyt])
                    for i in range(n // 128):
                        xt = xts[xi % 3]
                        kx = ("xt", xi % 3)
                        xi += 1
                        dma("sp", xt, x_src(l, b, tok0 + i * 128, 128), w=[kx])
                        for half in range(2):
                            for k in range(KC):
                                mm(bank(1 + half), yt[:, k, i * 128:(i + 1) * 128], w_out[:, k, half * 512:(half + 1) * 512],
                                   k == 0, k == KC - 1, [kyt, ("w_out", k)], [bk(1 + half)])
                        for half in range(2):
                            hs = slice(half * 512, (half + 1) * 512)
                            tt("dve", sq[:, hs], bank(1 + half), gr[:, hs], ALU.mult, [bk(1 + half), "grep"], ["sq"])
                            tt("dve", xt[:, hs], xt[:, hs], sq[:, hs], ALU.add, [kx, "sq"], [kx])
                        dma("pool", xs_d[b, tok0 + i * 128:tok0 + (i + 1) * 128, :], xt, r=[kx])
                        norm_modT(xt, sq, ms[:, 0:1], ms[:, 1:2], xn, h2[:, :, i * 128:(i + 1) * 128],
                                  A2[:, :, s], modT[:, 24:32, s], 3, kx, "ms", "xn", kh2)
                    dma("pool", h2T_d[b, :, :, tok0:tok0 + n].rearrange("c p t -> p c t"), h2[:, :, 0:n], r=[kh2])
            barrier()

            NJS = NJ // 2
            for sweep in range(2):
                ar = Arena(arena_t, arena_kb * 256)
                J0 = sweep * NJS
                wg = ar.alloc([KC, NJS * 128], BF16)
                wv = ar.alloc([KC, NJS * 128], BF16)
                wd = ar.alloc([NJS, D], BF16)
                for k in range(KC):
                    dma("pool", wg[:, k, :], w_gate_d[l, k * 128:(k + 1) * 128, J0 * 128:(J0 + NJS) * 128], w=[("wg", k)])
                    dma("pool", wv[:, k, :], w_val_d[l, k * 128:(k + 1) * 128, J0 * 128:(J0 + NJS) * 128], w=[("wv", k)])
                for j in range(NJS):
                    dma("pool", wd[:, j, :], w_down_d[l, (J0 + j) * 128:(J0 + j + 1) * 128, :], w=[("wd", j)])
                dgs = [ar.alloc([128], F32) for _ in range(2)]
                g2rep = [ar.alloc([D], F32) for _ in range(2)]
                h2in = [ar.alloc([KC, 514], BF16) for _ in range(2)]
                gts = [ar.alloc([514], F32) for _ in range(2)]
                acc = [ar.alloc([512], F32) for _ in range(2)]
                sg = [ar.alloc([512], F32) for _ in range(2)]
                aT = ar.alloc([NJS, 512], BF16)
                xts = [ar.alloc([D], F32) for _ in range(3)]
                tmp = ar.alloc([512], F32)
                for b in range(NB):
                    gvec_rep(g2rep[0], 40, b, 7)
                    if not last:
                        gvec_rep(g2rep[1], 40, NB, 7)
                    xi = 0
                    ji = 0
                    for si, (tok0, n, is_ctx, seg0, seglen) in enumerate(seg_tiles(b)):
                        if last and is_ctx:
                            continue
                        gr = g2rep[1 if is_ctx else 0]
                        hi_ = h2in[si % 2]
                        khi = ("h2in", si % 2)
                        lo = max(seg0, tok0 - 1)
                        hi = min(seg0 + seglen, tok0 + n + 1)
                        c0 = lo - (tok0 - 1)
                        if c0 > 0:
                            memset("dve", hi_[:, :, 0:c0], 0.0, [khi])
                        if hi < tok0 + n + 1:
                            memset("dve", hi_[:, :, n + 1:n + 2], 0.0, [khi])
                        dma("sp", hi_[:, :, c0:c0 + hi - lo], h2T_d[b, :, :, lo:hi].rearrange("c p t -> p c t"), w=[khi])
                        for j in range(NJS):
                            gb = ji % 2
                            g_ = gts[ji % 2]
                            a_ = acc[ji % 2]
                            s_ = sg[ji % 2]
                            kk_ = ji % 2
                            ji += 1
                            for k in range(KC):
                                mm(bank(gb)[:, 0:n], wg[:, k, j * 128:(j + 1) * 128], hi_[:, k, 1:n + 1],
                                   k == 0, k == KC - 1, [khi, ("wg", k)], [bk(gb)])
                            for k in range(KC):
                                mm(bank(4)[:, 0:2], wg[:, k, j * 128:(j + 1) * 128],
                                   hi_[:, k, 0:n + 2:n + 1], k == 0, k == KC - 1, [khi, ("wg", k)], [bk(4)])
                            for k in range(KC):
                                mm(bank(2 + gb)[:, 0:n], wv[:, k, j * 128:(j + 1) * 128], hi_[:, k, 1:n + 1],
                                   k == 0, k == KC - 1, [khi, ("wv", k)], [bk(2 + gb)])
                            cp("act", g_[:, 1:n + 1], bank(gb)[:, 0:n], [bk(gb)], [("gts", kk_)])
                            cp("act", g_[:, 0:n + 2:n + 1], bank(4)[:, 0:2], [bk(4)], [("gts", kk_)])
                            jj = J0 + j
                            ts("dve", a_[:, 0:n], g_[:, 0:n], fcw[:, jj, 0:1], ALU.mult, [("gts", kk_), "fcw"], [("acc", kk_)])
                            stt("dve", a_[:, 0:n], g_[:, 1:n + 1], fcw[:, jj, 1:2], a_[:, 0:n], ALU.mult, ALU.add,
                                [("gts", kk_), ("acc", kk_), "fcw"], [("acc", kk_)])
                            stt("dve", a_[:, 0:n], g_[:, 2:n + 2], fcw[:, jj, 2:3], a_[:, 0:n], ALU.mult, ALU.add,
                                [("gts", kk_), ("acc", kk_), "fcw"], [("acc", kk_)])
                            act(s_[:, 0:n], a_[:, 0:n], AF.Silu, [("acc", kk_), "fcb"], [("sg", kk_)], bias=fcb[:, jj:jj + 1])
                            tt("dve", aT[:, j, 0:n], s_[:, 0:n], bank(2 + gb)[:, 0:n], ALU.mult, [("sg", kk_), bk(2 + gb)], [("aT", j)])
                        for i in range(n // 128):
                            xt = xts[xi % 3]
                            kx = ("xt", xi % 3)
                            xi += 1
                            dma("sp", xt, xs_d[b, tok0 + i * 128:tok0 + (i + 1) * 128, :], w=[kx])
                            for half in range(2):
                                for j in range(NJS):
                                    mm(bank(5 + half), aT[:, j, i * 128:(i + 1) * 128], wd[:, j, half * 512:(half + 1) * 512],
                                       j == 0, j == NJS - 1, [("aT", j), ("wd", j)], [bk(5 + half)])
                            for half in range(2):
                                hs = slice(half * 512, (half + 1) * 512)
                                tt("dve", tmp, bank(5 + half), gr[:, hs], ALU.mult, [bk(5 + half), "grep"], ["tmp"])
                                tt("dve", xt[:, hs], xt[:, hs], tmp, ALU.add, [kx, "tmp"], [kx])
                            if last and sweep == 1:
                                dst = out_d[b, tok0 + i * 128:tok0 + (i + 1) * 128, :]
                            else:
                                dst = xs_d[b, tok0 + i * 128:tok0 + (i + 1) * 128, :]
                            dma("pool", dst, xt, r=[kx])
                barrier()

        S.emit(nc, st)
    return nc


def _strided2(ap2d, stride):
    return ap2d[:, 0:2 * stride].rearrange("p (a c) -> p a c", a=2)[:, :, 0]


def _rope_tables(L):
    ntl = L // 128
    tok = np.arange(L)
    row = (tok // GRID_W).astype(np.float32)
    col = (tok % GRID_W).astype(np.float32)
    inv = (10000.0 ** (-np.arange(0, 32, 2, dtype=np.float32) / 32)).astype(np.float32)
    ang = np.stack([row[:, None] * inv, col[:, None] * inv], axis=1).astype(np.float32)
    cos = np.cos(ang).astype(np.float32)
    sin = np.sin(ang).astype(np.float32)

    def lay(t):
        t = np.repeat(t[:, :, None, :], 2, axis=2).reshape(L, 64)
        return np.ascontiguousarray(t.reshape(ntl, 128, 64).transpose(1, 0, 2))
    return lay(cos), lay(sin)


def _fm(v, nchunk):
    sh = v.shape[:-1]
    return np.ascontiguousarray(np.swapaxes(v.reshape(*sh, nchunk, 128), -1, -2))


def make_in_maps(inputs, n_cores, NB, L, C):
    f = lambda a: np.ascontiguousarray(np.asarray(a, dtype=np.float32))
    x = f(inputs["x"]); c = f(inputs["c"]); ctx = f(inputs["ctx"]); c_ctx = f(inputs["c_ctx"])
    depth = inputs["w_mod"].shape[0]
    rcos, rsin = _rope_tables(L)
    shared = {
        "w_mod": f(inputs["w_mod"]),
        "b_modT": _fm(f(inputs["b_mod"]), 48),
        "norm1_gT": _fm(f(inputs["norm1_g"]), KC),
        "norm2_gT": _fm(f(inputs["norm2_g"]), KC),
        "w_in": f(inputs["w_in"]),
        "ln_v": np.ascontiguousarray(np.stack([f(inputs["ln_v_g"]), f(inputs["ln_v_b"])], axis=1)),
        "w_sT": np.ascontiguousarray(f(inputs["w_s"]).transpose(0, 1, 3, 2)),
        "b_sT": np.ascontiguousarray(f(inputs["b_s"]).transpose(0, 2, 1)),
        "qk_g": np.ascontiguousarray(np.stack([f(inputs["q_norm_g"]), f(inputs["k_norm_g"])], axis=1)),
        "lam": np.ascontiguousarray(np.stack([f(inputs["lam_q1"]), f(inputs["lam_k1"]),
                                              f(inputs["lam_q2"]), f(inputs["lam_k2"])], axis=1)),
        "subln_gT": np.ascontiguousarray(f(inputs["subln_g"])[:, :, None]),
        "conv_wT": np.ascontiguousarray(f(inputs["conv_w"]).reshape(depth, CONVK, 2, 128).transpose(0, 3, 2, 1)),
        "c_vecT": np.ascontiguousarray(np.stack([f(inputs["conv_b"]), f(inputs["ln_c_g"]), f(inputs["ln_c_b"])],
                                                axis=1).reshape(depth, 3, 2, 128).transpose(0, 3, 1, 2)),
        "w_out": f(inputs["w_out"]),
        "w_gate": f(inputs["w_gate"]),
        "w_val": f(inputs["w_val"]),
        "ffn_conv_wT": np.ascontiguousarray(f(inputs["ffn_conv_w"]).reshape(depth, 3, NJ, 128).transpose(0, 3, 2, 1)),
        "ffn_conv_bT": _fm(f(inputs["ffn_conv_b"]), NJ),
        "w_down": f(inputs["w_down"]),
        "rope_cos": rcos,
        "rope_sin": rsin,
    }
    maps = []
    for i in range(n_cores):
        bs = slice(i * NB, (i + 1) * NB)
        cv = np.concatenate([c[bs], c_ctx[None, :]], axis=0)
        cT = np.ascontiguousarray(cv.reshape(NB + 1, KC, 128).transpose(2, 1, 0))
        m = dict(shared)
        m["x"] = np.ascontiguousarray(x[bs])
        m["ctx"] = np.ascontiguousarray(ctx[bs])
        m["cT"] = cT
        maps.append(m)
    return maps


_NC_CACHE = {}


def kernel(**inputs):
    x = inputs["x"]
    B, L, _ = x.shape
    C = inputs["ctx"].shape[1]
    depth = inputs["w_mod"].shape[0]
    NB = B // N_CORES
    key = (L, C, NB, depth)
    if key not in _NC_CACHE:
        _NC_CACHE[key] = build_program(L, C, NB, depth)
    nc = _NC_CACHE[key]
    in_maps = make_in_maps(inputs, N_CORES, NB, L, C)
    res = run_bass_kernel_spmd(nc, in_maps, core_ids=list(range(N_CORES)))
    out = np.concatenate([np.asarray(r["out"], dtype=np.float32) for r in res.results], axis=0)
    return out.reshape(B, L, D)
```

```python
import contextlib
import math
import numpy as np
import concourse.bass as bass
import concourse.mybir as mybir
from concourse.bass_utils import run_bass_kernel_spmd

F32 = mybir.dt.float32
BF16 = mybir.dt.bfloat16
AF = mybir.ActivationFunctionType
ALU = mybir.AluOpType
AX = mybir.AxisListType

D = 1024
KC = 8
GRID_W = 64
EPS = 1e-6
IN_COLS = 2560
COL_Q, COL_K, COL_V, COL_C = 512, 1024, 1536, 2048
DFF = 2816
NJ = 22
CONVK = 31
N_CORES = 8

ENGS = ("pe", "act", "dve", "pool", "sp")
EPOCH = 20000
N_DMA_SEMS = 24
ROPE_ENG = "pool"


class Op:
    __slots__ = ("eng", "fn", "deps", "signal", "sig", "dma", "idx", "prev_dma")

    def __init__(self, eng, fn, dma, idx):
        self.eng = eng
        self.fn = fn
        self.dma = dma
        self.deps = []
        self.signal = dma
        self.sig = None
        self.idx = idx
        self.prev_dma = None


class Sched:
    def __init__(self):
        self.ops = []
        self.lastw = {}
        self.readers = {}
        self.bar = None
        self.last_eng = {}
        self.dma_rr = {e: 0 for e in ENGS}
        self.dma_cnt = {}
        self.dma_last = {}

    def _link(self, op, d):
        if d is op:
            return
        if d.dma:
            op.deps.append(d)
        elif d.eng != op.eng:
            d.signal = True
            op.deps.append(d)
        elif op.dma or op.eng != "pe":
            d.signal = True
            op.deps.append(d)

    def add(self, eng, fn, r=(), w=(), dma=False):
        op = Op(eng, fn, dma, len(self.ops))
        deps = {}
        if self.bar is not None:
            deps[self.bar.idx] = self.bar
        for k in r:
            d = self.lastw.get(k)
            if d is not None:
                deps[d.idx] = d
        for k in w:
            d = self.lastw.get(k)
            if d is not None:
                deps[d.idx] = d
            for rd in self.readers.get(k, ()):
                deps[rd.idx] = rd
        for d in deps.values():
            self._link(op, d)
        for k in r:
            self.readers.setdefault(k, []).append(op)
        for k in w:
            self.lastw[k] = op
            self.readers[k] = []
        if dma:
            k = self.dma_rr[eng] % N_DMA_SEMS
            self.dma_rr[eng] += 1
            key = ("dma", eng, k)
            self.dma_cnt[key] = self.dma_cnt.get(key, 0) + 16
            op.sig = (key, self.dma_cnt[key])
            op.prev_dma = self.dma_last.get(key)
            self.dma_last[key] = op
        else:
            self.last_eng[eng] = op
        self.ops.append(op)
        return op

    def barrier(self, fn):
        op = Op("dve", fn, False, len(self.ops))
        for e, d in self.last_eng.items():
            self._link(op, d)
        for d in self.dma_last.values():
            op.deps.append(d)
        self.last_eng["dve"] = op
        self.ops.append(op)
        self.bar = op
        self.lastw = {}
        self.readers = {}
        return op

    def emit(self, nc, stack):
        cnt = {e: 0 for e in ENGS}
        n_epochs = {e: 1 for e in ENGS}
        for op in self.ops:
            if (not op.dma) and op.signal:
                cnt[op.eng] += 1
                ep = (cnt[op.eng] - 1) // EPOCH
                n_epochs[op.eng] = ep + 1
                op.sig = (("c", op.eng, ep), cnt[op.eng] - ep * EPOCH)
        sems = {}

        def get_sem(key):
            if key not in sems:
                sems[key] = stack.enter_context(nc.semaphore("s_" + "_".join(str(x) for x in key)))
            return sems[key]

        for e in ENGS:
            for ep in range(n_epochs[e]):
                get_sem(("c", e, ep))
        for key in self.dma_cnt:
            get_sem(key)
        per_eng = {e: [op for op in self.ops if op.eng == e] for e in ENGS}
        dma_last = self.dma_last

        def run_engine(ename, eng):
            waited = {}
            for op in per_eng[ename]:
                need = {}
                for d in op.deps:
                    key, val = d.sig
                    if waited.get(key, 0) < val and need.get(key, 0) < val:
                        need[key] = val
                if op.dma and op.prev_dma is not None:
                    key, val = op.prev_dma.sig
                    if waited.get(key, 0) < val and need.get(key, 0) < val:
                        need[key] = val
                for key, val in need.items():
                    eng.wait_ge(get_sem(key), val)
                    waited[key] = val
                inst = op.fn(eng)
                if op.dma:
                    inst.then_inc(get_sem(op.sig[0]), 16)
                elif op.signal:
                    inst.then_inc(get_sem(op.sig[0]), 1)
            for key, op in dma_last.items():
                if key[1] == ename:
                    k, val = op.sig
                    if waited.get(k, 0) < val:
                        eng.wait_ge(get_sem(k), val)

        block = stack.enter_context(nc.Block())

        @block.tensor
        def _(e):
            run_engine("pe", e)

        @block.scalar
        def _(e):
            run_engine("act", e)

        @block.vector
        def _(e):
            run_engine("dve", e)

        @block.gpsimd
        def _(e):
            run_engine("pool", e)

        @block.sync
        def _(e):
            run_engine("sp", e)


class Arena:
    def __init__(self, t, nwords):
        self.t = t
        self.n = nwords
        self.off = 0

    def alloc(self, free_shape, dt):
        esz = 2 if dt == BF16 else 4
        nel = 1
        for s in free_shape:
            nel *= s
        nw = (nel * esz + 3) // 4
        nw = (nw + 7) // 8 * 8
        assert self.off + nw <= self.n, ("arena overflow", self.off, nw, self.n)
        v = self.t[:, self.off:self.off + nw]
        self.off += nw
        if dt == BF16:
            v = v.bitcast(BF16)
        v = v[:, 0:nel]
        if len(free_shape) == 2:
            v = v.rearrange("p (a b) -> p a b", a=free_shape[0])
        elif len(free_shape) == 3:
            v = v.rearrange("p (a b c) -> p a b c", a=free_shape[0], b=free_shape[1])
        return v


def build_program(L, C, NB, DEPTH, arena_kb=160):
    T = L + C
    NT = T // 128
    NTL = L // 128
    nc = bass.Bass("TRN2", target_bir_lowering=False)
    S = Sched()

    def din(name, shape, dt=F32):
        return nc.dram_tensor(name, list(shape), dt, kind="ExternalInput").ap()

    x_in = din("x", [NB, L, D])
    ctx_in = din("ctx", [NB, C, D])
    cT_d = din("cT", [128, KC, NB + 1])
    w_mod_d = din("w_mod", [DEPTH, D, 6 * D])
    b_modT_d = din("b_modT", [DEPTH, 128, 48])
    n1g_d = din("norm1_gT", [DEPTH, 128, KC])
    n2g_d = din("norm2_gT", [DEPTH, 128, KC])
    w_in_d = din("w_in", [DEPTH, D, IN_COLS])
    lnv_d = din("ln_v", [DEPTH, 2, 256])
    w_sT_d = din("w_sT", [DEPTH, 4, 128, 128])
    b_sT_d = din("b_sT", [DEPTH, 128, 4])
    qkg_d = din("qk_g", [DEPTH, 2, 64])
    lam_d = din("lam", [DEPTH, 4, 64])
    subg_d = din("subln_gT", [DEPTH, 128, 1])
    convw_d = din("conv_wT", [DEPTH, 128, 2, CONVK])
    cvec_d = din("c_vecT", [DEPTH, 128, 3, 2])
    w_out_d = din("w_out", [DEPTH, D, D])
    w_gate_d = din("w_gate", [DEPTH, D, DFF])
    w_val_d = din("w_val", [DEPTH, D, DFF])
    fcw_d = din("ffn_conv_wT", [DEPTH, 128, NJ, 3])
    fcb_d = din("ffn_conv_bT", [DEPTH, 128, NJ])
    w_down_d = din("w_down", [DEPTH, DFF, D])
    rcos_d = din("rope_cos", [128, NTL, 64])
    rsin_d = din("rope_sin", [128, NTL, 64])
    out_d = nc.dram_tensor("out", [NB, L, D], F32, kind="ExternalOutput").ap()

    xs_d = nc.dram_tensor("xs", [NB, T, D], F32).ap()
    ycat_d = nc.dram_tensor("ycat", [NB, 8, 128, T], BF16).ap()
    h2T_d = nc.dram_tensor("h2T", [NB, 8, 128, T], BF16).ap()
    uc_d = nc.dram_tensor("uc", [NB, 2, 128, T], BF16).ap()
    qt_d = nc.dram_tensor("qt", [NB, 4, 128, T], BF16).ap()
    vt_d = nc.dram_tensor("vt", [NB, NT, 128, 512], BF16).ap()

    st = contextlib.ExitStack()
    with st:
        def sb(name, shape, dt=F32):
            return st.enter_context(nc.sbuf_tensor(name, list(shape), dt))

        identf = sb("identf", [128, 128])
        identb = sb("identb", [128, 128], BF16)
        iot = sb("iot", [128, 128])
        onesb = sb("onesb", [128, 128], BF16)
        onesf = sb("onesf", [128, 128])
        ones256 = sb("ones256", [128, 128])
        ones128 = sb("ones128", [128, 128])
        barsc = sb("barsc", [128, 1])
        rcos = sb("rcos", [128, NTL, 64])
        rsin = sb("rsin", [128, NTL, 64])
        cTs = sb("cTs", [128, KC, NB + 1])
        siluT = sb("siluT", [128, KC, NB + 1])
        modT = sb("modT", [128, 48, NB + 1])
        bmod = sb("bmod", [128, 48])
        n1g = sb("n1g", [128, KC])
        n2g = sb("n2g", [128, KC])
        A1 = sb("A1", [128, KC, NB + 1])
        A2 = sb("A2", [128, KC, NB + 1])
        lnvg = sb("lnvg", [128, 256])
        lnvb = sb("lnvb", [128, 256])
        bsT = sb("bsT", [128, 4])
        gq = sb("gq", [128, 64])
        gk = sb("gk", [128, 64])
        lamv = sb("lamv", [128, 4, 64])
        lamp = sb("lamp", [128, 2, 64])
        lams = sb("lams", [128, 2])
        neglam = sb("neglam", [128, 1])
        subg = sb("subg", [128, 1])
        convw = sb("convw", [128, 2, CONVK])
        cvec = sb("cvec", [128, 3, 2])
        fcw = sb("fcw", [128, NJ, 3])
        fcb = sb("fcb", [128, NJ])
        wsT = sb("wsT", [128, 4, 128], BF16)
        wsTf = sb("wsTf", [128, 4, 128])
        arena_t = sb("arena", [128, arena_kb * 256])
        pp = st.enter_context(nc.psum_tensor("pp", [128, 8, 512], F32))

        def bank(i):
            return pp[:, i, :]

        def bankb(i):
            return pp[:, i, :].bitcast(BF16)

        def bk(i):
            return ("bank", i)

        def mm(out, lhsT, rhs, start, stop, r, w):
            S.add("pe", lambda e: e.matmul(out, lhsT=lhsT, rhs=rhs, start=start, stop=stop), r, w)

        def tr(out, in_, r, w):
            S.add("pe", lambda e: e.transpose(out=out, in_=in_, identity=identb[:]), r, w)

        def act(out, in_, func, r, w, scale=1.0, bias=0.0, accum=None):
            if accum is None:
                S.add("act", lambda e: e.activation(out=out, in_=in_, func=func, bias=bias, scale=scale), r, w)
            else:
                S.add("act", lambda e: e.activation(out=out, in_=in_, func=func, bias=bias, scale=scale,
                                                    accum_out=accum), r, w)

        def tt(eng, out, in0, in1, op, r, w):
            S.add(eng, lambda e: e.tensor_tensor(out=out, in0=in0, in1=in1, op=op), r, w)

        def ts(eng, out, in0, s1, op0, r, w, s2=None, op1=None):
            if op1 is None:
                S.add(eng, lambda e: e.tensor_scalar(out=out, in0=in0, scalar1=s1, scalar2=None, op0=op0), r, w)
            else:
                S.add(eng, lambda e: e.tensor_scalar(out=out, in0=in0, scalar1=s1, scalar2=s2, op0=op0, op1=op1), r, w)

        def stt(eng, out, in0, scalar, in1, op0, op1, r, w):
            S.add(eng, lambda e: e.scalar_tensor_tensor(out=out, in0=in0, scalar=scalar, in1=in1, op0=op0, op1=op1), r, w)

        def cp(eng, out, in_, r, w):
            if eng == "act":
                S.add("act", lambda e: e.copy(out=out, in_=in_), r, w)
            else:
                S.add(eng, lambda e: e.tensor_copy(out=out, in_=in_), r, w)

        def memset(eng, ap, val, w):
            S.add(eng, lambda e: e.memset(ap, val), (), w)

        def dma(eng, out, in_, r=(), w=()):
            S.dma_op = S.add(eng, lambda e: e.dma_start(out=out, in_=in_), r, w, dma=True)

        def barrier():
            S.barrier(lambda e: e.memset(barsc[:], 0.0))

        def rsqrt_act(out, in_, r, w, scale=1.0, eps=EPS):
            act(out, in_, AF.Ln, r, w, scale=scale, bias=epsb[:, 0:1] if eps == EPS else eps)
            act(out, out, AF.Exp, w, w, scale=-0.5)

        epsb = sb("epsb", [128, 1])

        memset("dve", epsb[:], EPS, ["epsb"])
        S.add("pool", lambda e: e.iota(iot[:], pattern=[[1, 128]], base=0, channel_multiplier=-1,
                                       allow_small_or_imprecise_dtypes=True), (), ["iot"])
        S.add("dve", lambda e: e.tensor_single_scalar(out=identf[:], in_=iot[:], scalar=0.0, op=ALU.is_equal),
              ["iot"], ["identf"])
        cp("dve", identb[:], identf[:], ["identf"], ["identb"])
        memset("dve", onesb[:], 1.0, ["onesb"])
        memset("dve", onesf[:], 1.0, ["onesf"])
        memset("dve", ones256[:], 1.0 / 256, ["ones256"])
        memset("dve", ones128[:], 1.0 / 128, ["ones128"])
        dma("sp", rcos[:], rcos_d, w=["rcos"])
        dma("sp", rsin[:], rsin_d, w=["rsin"])
        dma("sp", cTs[:], cT_d, w=["cTs"])
        act(siluT[:], cTs[:], AF.Silu, ["cTs"], ["siluT"])
        barrier()

        def seg_tiles(b):
            res = []
            for s0 in range(0, L, 512):
                res.append((s0, min(512, L - s0), False, 0, L))
            for s0 in range(0, C, 512):
                res.append((L + s0, min(512, C - s0), True, L, C))
            return res

        def x_src(l, b, tok0, n):
            if l == 0:
                if tok0 < L:
                    return x_in[b, tok0:tok0 + n, :]
                return ctx_in[b, tok0 - L:tok0 - L + n, :]
            return xs_d[b, tok0:tok0 + n, :]

        for l in range(DEPTH):
            last = (l == DEPTH - 1)
            lam_init = 0.8 - 0.6 * math.exp(-0.3 * l)
            dma("sp", bmod[:], b_modT_d[l], w=["bmod"])
            dma("sp", n1g[:], n1g_d[l], w=["n1g"])
            dma("sp", n2g[:], n2g_d[l], w=["n2g"])
            dma("sp", lnvg[:], lnv_d[l, 0:1, :].broadcast_to([128, 256]), w=["lnvg"])
            dma("sp", lnvb[:], lnv_d[l, 1:2, :].broadcast_to([128, 256]), w=["lnvb"])
            dma("sp", bsT[:], b_sT_d[l], w=["bsT"])
            dma("sp", gq[:], qkg_d[l, 0:1, :].broadcast_to([128, 64]), w=["gq"])
            dma("sp", gk[:], qkg_d[l, 1:2, :].broadcast_to([128, 64]), w=["gk"])
            for i in range(4):
                dma("sp", lamv[:, i, :], lam_d[l, i:i + 1, :].broadcast_to([128, 64]), w=[("lamv", i)])
            dma("sp", subg[:], subg_d[l], w=["subg"])
            dma("sp", convw[:], convw_d[l], w=["convw"])
            dma("sp", cvec[:], cvec_d[l], w=["cvec"])
            dma("sp", fcw[:], fcw_d[l], w=["fcw"])
            dma("sp", fcb[:], fcb_d[l], w=["fcb"])
            dma("sp", wsTf[:], w_sT_d[l].rearrange("h q p -> q h p"), w=["wsTf"])
            cp("dve", wsT[:], wsTf[:], ["wsTf"], ["wsT"])
            ts("dve", gq[:], gq[:], 0.125, ALU.mult, ["gq"], ["gq"])
            tt("dve", lamp[:, 0, :], lamv[:, 0, :], lamv[:, 1, :], ALU.mult, [("lamv", 0), ("lamv", 1)], ["lamp0"])
            tt("dve", lamp[:, 1, :], lamv[:, 2, :], lamv[:, 3, :], ALU.mult, [("lamv", 2), ("lamv", 3)], ["lamp1"])
            S.add("dve", lambda e: e.tensor_reduce(out=lams[:], in_=lamp[:], axis=AX.X, op=ALU.add),
                  ["lamp0", "lamp1"], ["lams"])
            act(lams[:], lams[:], AF.Exp, ["lams"], ["lams"])
            tt("dve", neglam[:], lams[:, 1:2], lams[:, 0:1], ALU.subtract, ["lams"], ["neglam"])
            ts("dve", neglam[:], neglam[:], -lam_init, ALU.add, ["neglam"], ["neglam"])
            ts("dve", subg[:], subg[:], 1.0 - lam_init, ALU.mult, ["subg"], ["subg"])

            ar = Arena(arena_t, arena_kb * 256)
            NBLK = 8
            BW = 6 * D // NBLK
            wm = [ar.alloc([KC, BW], F32) for _ in range(2)]
            for blk in range(NBLK):
                wt = wm[blk % 2]
                for k in range(KC):
                    dma("sp", wt[:, k, :], w_mod_d[l, k * 128:(k + 1) * 128, blk * BW:(blk + 1) * BW],
                        w=[("wm", blk % 2, k)])
                pbi = blk % 2
                nj = BW // 128
                for jj in range(nj):
                    for k in range(KC):
                        mm(bank(pbi)[:, jj * 4:jj * 4 + NB + 1], wt[:, k, jj * 128:(jj + 1) * 128], siluT[:, k, :],
                           k == 0, k == KC - 1, [("wm", blk % 2, k), "siluT"], [bk(pbi)])
                j0 = blk * nj
                tt("dve", modT[:, j0:j0 + nj, :],
                   bank(pbi)[:, 0:nj * 4].rearrange("p (j c) -> p j c", c=4)[:, :, 0:NB + 1],
                   bmod[:, j0:j0 + nj].unsqueeze(2).broadcast_to([128, nj, NB + 1]), ALU.add,
                   [bk(pbi), "bmod"], ["modT"])
            stt("dve", A1[:], modT[:, 8:16, :], 1.0, n1g[:].unsqueeze(2).broadcast_to([128, KC, NB + 1]),
                ALU.add, ALU.mult, ["modT", "n1g"], ["A1"])
            stt("dve", A2[:], modT[:, 32:40, :], 1.0, n2g[:].unsqueeze(2).broadcast_to([128, KC, NB + 1]),
                ALU.add, ALU.mult, ["modT", "n2g"], ["A2"])
            barrier()

            def stream_of(b, is_ctx):
                return NB if is_ctx else b

            def norm_modT(xt, sq, ms, rstd, xn, hT_out, Acol, Bcol, pbank, kx, kms, kxn, khT):
                act(sq, xt, AF.Square, [kx], ["sq", kms], scale=1.0 / 32, accum=ms)
                rsqrt_act(rstd, ms, [kms], [kms])
                ts("dve", xn, xt, rstd, ALU.mult, [kx, kms], [kxn])
                pb = bankb(pbank).rearrange("p (k n) -> p k n", k=KC)
                for k in range(KC):
                    tr(pb[:, k, :], xn[:, k * 128:(k + 1) * 128], [kxn], [bk(pbank)])
                for k in range(KC):
                    act(hT_out[:, k, :], pb[:, k, :], AF.Identity, [bk(pbank), "A", "modT"], [khT],
                        scale=Acol[:, k:k + 1], bias=Bcol[:, k:k + 1])

            def gvec_rep(dst, jbase, s, pbank):
                for half in range(2):
                    for kk in range(4):
                        k = half * 4 + kk
                        dg = dgs[k % 2]
                        ts("dve", dg, identf[:], modT[:, jbase + k, s:s + 1], ALU.mult, [], [("dg", k % 2)])
                        mm(bank(pbank)[:, kk * 128:(kk + 1) * 128], onesf[:], dg, True, True,
                           [("dg", k % 2)], [bk(pbank)])
                    cp("dve", dst[:, half * 512:(half + 1) * 512], bank(pbank), [bk(pbank)], ["grep"])

            for b in range(NB):
                ar = Arena(arena_t, arena_kb * 256)
                KT = ar.alloc([4, T], BF16)
                w_in = ar.alloc([KC, IN_COLS], BF16)
                if b == 0:
                    for k in range(KC):
                        dma("pool", w_in[:, k, :], w_in_d[l, k * 128:(k + 1) * 128, :], w=[("w_in", k)])
                p1_mark = ar.off
                xts = [ar.alloc([D], F32) for _ in range(2)]
                sq = ar.alloc([D], F32)
                ms = ar.alloc([2], F32)
                xn = ar.alloc([D], BF16)
                hTs = [ar.alloc([KC, 512], BF16) for _ in range(2)]
                Vsts = [ar.alloc([4, 512], BF16) for _ in range(2)]
                sig = ar.alloc([512], F32)
                ucs = [ar.alloc([2, 512], BF16) for _ in range(2)]
                zts = [ar.alloc([512], F32) for _ in range(2)]
                bnst = ar.alloc([8], F32)
                vn = ar.alloc([256], F32)
                vnbs = [ar.alloc([256], BF16) for _ in range(2)]
                yas = [ar.alloc([256], BF16) for _ in range(2)]
                yaTs = [ar.alloc([2, 512], BF16) for _ in range(2)]
                qsqs = [ar.alloc([512], F32) for _ in range(2)]
                stt_ = ar.alloc([24], F32)
                qns = [ar.alloc([512], F32) for _ in range(2)]
                qcss = [ar.alloc([1024], F32) for _ in range(2)]
                qrbs = [[ar.alloc([512], BF16) for _ in range(2)] for _ in range(2)]
                QTs = [ar.alloc([4, 512], BF16) for _ in range(2)]
                wk = [("w_in", k) for k in range(KC)]
                tiles = seg_tiles(b)
                xi_box = [0]

                def emit_norm(si, i):
                    tok0, n, is_ctx, seg0, seglen = tiles[si]
                    s = stream_of(b, is_ctx)
                    xt = xts[xi_box[0] % 2]
                    kx = ("xt", xi_box[0] % 2)
                    xi_box[0] += 1
                    dma("sp", xt, x_src(l, b, tok0 + i * 128, 128), w=[kx])
                    norm_modT(xt, sq, ms[:, 0:1], ms[:, 1:2], xn, hTs[si % 2][:, :, i * 128:(i + 1) * 128],
                              A1[:, :, s], modT[:, 0:8, s], 0, kx, "ms", "xn", ("hT", si % 2))

                def emit_blocks(si, i):
                    tok0, n, is_ctx, seg0, seglen = tiles[si]
                    hT = hTs[si % 2]
                    khT = ("hT", si % 2)
                    ti = (tok0 + i * 128) // 128
                    par = i % 2
                    zt = zts[par]
                    vnb = vnbs[par]
                    hasq = not (last and is_ctx)
                    lhs = [hT[:, k, i * 128:(i + 1) * 128] for k in range(KC)]
                    chains = [(1, COL_K, gk, 1, 0)] + ([(0, COL_Q, gq, 6, 1)] if hasq else [])
                    nst = 2 + 8 * len(chains)
                    v3 = lambda a: a.rearrange("p (g d) -> p g d", g=8)
                    v5 = lambda a: a.rearrange("p (g a h f) -> p g a h f", g=8, a=2, h=2)
                    for k in range(KC):
                        mm(bank(3), lhs[k], w_in[:, k, 0:512], k == 0, k == KC - 1, [khT, wk[k]], [bk(3)])
                    for which, col, g, pbi, sc in chains:
                        for k in range(KC):
                            mm(bank(pbi), lhs[k], w_in[:, k, col:col + 512], k == 0, k == KC - 1,
                               [khT, wk[k]], [bk(pbi)])
                    for k in range(KC):
                        mm(bank(2), lhs[k], w_in[:, k, COL_V:COL_V + 512], k == 0, k == KC - 1,
                           [khT, wk[k]], [bk(2)])
                    act(zt, bank(3), AF.Gelu_apprx_tanh, [bk(3)], [("zt", par)])
                    for which, col, g, pbi, sc in chains:
                        act(qsqs[sc], bank(pbi), AF.Square, [bk(pbi)], [("qsq", sc)], scale=0.125)
                    cp("act", Vsts[si % 2][:, i, :], bank(2), [bk(2)], [("Vst", si % 2)])
                    S.add("dve", lambda e, bnst=bnst, zt=zt: e.bn_stats(out=bnst[:, 0:6], in_=zt[:, 256:512]),
                          [("zt", par)], ["bnst"])
                    S.add("dve", lambda e, bnst=bnst, stt_=stt_: e.bn_aggr(out=stt_[:, 0:2], in_=bnst[:, 0:6]),
                          ["bnst"], ["st"])
                    for which, col, g, pbi, sc in chains:
                        S.add("dve", lambda e, sc=sc, stt_=stt_: e.tensor_reduce(
                            out=stt_[:, 2 + 8 * sc:10 + 8 * sc], in_=qsqs[sc].rearrange("p (g d) -> p g d", g=8),
                            axis=AX.X, op=ALU.add), [("qsq", sc)], ["st"])
                    rsqrt_act(stt_[:, 1:nst], stt_[:, 1:nst], ["st"], ["st"])
                    ts("dve", vn, zt[:, 256:512], stt_[:, 0:1], ALU.subtract, [("zt", par), "st"], ["vn"],
                       s2=stt_[:, 1:2], op1=ALU.mult)
                    tt("dve", vn, vn, lnvg[:], ALU.mult, ["vn", "lnvg"], ["vn"])
                    tt("dve", vnb, vn, lnvb[:], ALU.add, ["vn", "lnvb"], [("vnb", par)])
                    for which, col, g, pbi, sc in chains:
                        qn_ = qns[sc]
                        qc = qcss[sc][:, 0:512]
                        qs_ = qcss[sc][:, 512:1024]
                        qrb = qrbs[which][par]
                        kqrb = ("qrb", which, par)
                        tt("dve", v3(qn_), v3(bank(pbi)),
                           stt_[:, 2 + 8 * sc:10 + 8 * sc].unsqueeze(2).broadcast_to([128, 8, 64]), ALU.mult,
                           [bk(pbi), "st"], [("qn", sc)])
                        gb = g[:].unsqueeze(1).broadcast_to([128, 8, 64])
                        if is_ctx:
                            tt("dve", v3(qrb), v3(qn_), gb, ALU.mult, [("qn", sc), "g"], [kqrb])
                        else:
                            tt("dve", v3(qn_), v3(qn_), gb, ALU.mult, [("qn", sc), "g"], [("qn", sc)])
                            cb = rcos[:, ti, :].unsqueeze(1).broadcast_to([128, 8, 64])
                            sbb = rsin[:, ti, :].unsqueeze(1).broadcast_to([128, 8, 64])
                            tt(ROPE_ENG, v3(qc), v3(qn_), cb, ALU.mult, [("qn", sc), "rcos"], [("qcs", sc)])
                            tt(ROPE_ENG, v3(qs_), v3(qn_), sbb, ALU.mult, [("qn", sc), "rsin"], [("qcs", sc)])
                            for ax in range(2):
                                tt(ROPE_ENG, v5(qrb)[:, :, ax, 0, :], v5(qc)[:, :, ax, 0, :], v5(qs_)[:, :, ax, 1, :],
                                   ALU.subtract, [("qcs", sc)], [kqrb])
                                tt(ROPE_ENG, v5(qrb)[:, :, ax, 1, :], v5(qc)[:, :, ax, 1, :], v5(qs_)[:, :, ax, 0, :],
                                   ALU.add, [("qcs", sc)], [kqrb])

                def emit_dep(si, i):
                    tok0, n, is_ctx, seg0, seglen = tiles[si]
                    ti = (tok0 + i * 128) // 128
                    par = i % 2
                    zt = zts[par]
                    vnb = vnbs[par]
                    ya = yas[par]
                    yaT = yaTs[si % 2]
                    for h in range(4):
                        mm(bank(4)[:, h * 64:(h + 1) * 64], wsT[:, h, :], vnb[:, h * 64:(h + 1) * 64],
                           True, True, [("vnb", par), "wsT"], [bk(4)])
                    for h in range(4):
                        stt("dve", ya[:, h * 64:(h + 1) * 64], bank(4)[:, h * 64:(h + 1) * 64], bsT[:, h:h + 1],
                            zt[:, h * 64:(h + 1) * 64], ALU.add, ALU.mult, [bk(4), ("zt", par), "bsT"], [("ya", par)])
                    pbT = bankb(5).rearrange("p (k n) -> p k n", k=8)
                    for cj in range(2):
                        tr(pbT[:, cj, :], ya[:, cj * 128:(cj + 1) * 128], [("ya", par)], [bk(5)])
                    cp("dve", yaT[:, :, i * 128:(i + 1) * 128], pbT[:, 0:2, :], [bk(5)], [("yaT", si % 2)])
                    for which in range(2):
                        if which == 0 and last and is_ctx:
                            continue
                        qrb = qrbs[which][par]
                        pbT = bankb(7).rearrange("p (k n) -> p k n", k=8)
                        for h in range(4):
                            tr(pbT[:, h, :], qrb[:, h * 128:(h + 1) * 128], [("qrb", which, par)], [bk(7)])
                        if which == 0:
                            cp("act", QTs[si % 2][:, :, i * 128:(i + 1) * 128], pbT[:, 0:4, :], [bk(7)], [("QTs", si % 2)])
                        else:
                            cp("act", KT[:, :, ti * 128:(ti + 1) * 128], pbT[:, 0:4, :], [bk(7)], [("KT", ti)])

                for i in range(tiles[0][1] // 128):
                    emit_norm(0, i)
                for si, (tok0, n, is_ctx, seg0, seglen) in enumerate(tiles):
                    hT = hTs[si % 2]
                    khT = ("hT", si % 2)
                    nt = n // 128
                    nt_next = tiles[si + 1][1] // 128 if si + 1 < len(tiles) else 0
                    uc = ucs[si % 2]
                    kuc = ("ucs", si % 2)
                    if not (last and is_ctx):
                        for cj in range(2):
                            for part, pbi in ((0, 1), (1, 2)):
                                col = COL_C + part * 256 + cj * 128
                                for k in range(KC):
                                    mm(bank(pbi)[:, 0:n], w_in[:, k, col:col + 128], hT[:, k, 0:n],
                                       k == 0, k == KC - 1, [khT, wk[k]], [bk(pbi)])
                            act(sig[:, 0:n], bank(2)[:, 0:n], AF.Sigmoid, [bk(2)], ["sig"])
                            tt("dve", uc[:, cj, 0:n], bank(1)[:, 0:n], sig[:, 0:n], ALU.mult, [bk(1), "sig"], [kuc])
                        dma("pool", uc_d[b, :, :, tok0:tok0 + n].rearrange("c p t -> p c t"), uc[:, :, 0:n], r=[kuc])
                    for i in range(nt):
                        emit_blocks(si, i)
                        if i > 0:
                            emit_dep(si, i - 1)
                        if i < nt_next:
                            emit_norm(si + 1, i)
                    emit_dep(si, nt - 1)
                    for i in range(nt, nt_next):
                        emit_norm(si + 1, i)
                    dma("pool", ycat_d[b, 0:2, :, tok0:tok0 + n].rearrange("c p t -> p c t"),
                        yaTs[si % 2][:, :, 0:n], r=[("yaT", si % 2)])
                    if not (last and is_ctx):
                        dma("pool", qt_d[b, :, :, tok0:tok0 + n].rearrange("c p t -> p c t"),
                            QTs[si % 2][:, :, 0:n], r=[("QTs", si % 2)])
                    dma("pool", vt_d[b, tok0 // 128:tok0 // 128 + nt, :, :].rearrange("t p c -> p t c"),
                        Vsts[si % 2][:, 0:nt, :], r=[("Vst", si % 2)])
                barrier()

                ar.off = p1_mark
                dg = ar.alloc([2 * CONVK, 128], BF16)
                for cj in range(2):
                    for k in range(CONVK):
                        ts("dve", dg[:, cj * CONVK + k, :], identf[:], convw[:, cj, k:k + 1], ALU.mult, [], ["dgc"])
                HAL = CONVK // 2
                ucin = [ar.alloc([2, 512 + 2 * HAL], BF16) for _ in range(2)]
                yc = [ar.alloc([512], F32) for _ in range(2)]
                ysq = ar.alloc([512], F32)
                m2 = ar.alloc([512], F32)
                rstdc = ar.alloc([512], F32)
                tmpc = ar.alloc([512], F32)
                ycb = [ar.alloc([2, 512], BF16) for _ in range(2)]
                for si, (tok0, n, is_ctx, seg0, seglen) in enumerate(seg_tiles(b)):
                    if last and is_ctx:
                        continue
                    ui = ucin[si % 2]
                    kui = ("ucin", si % 2)
                    lo = max(seg0, tok0 - HAL)
                    hi = min(seg0 + seglen, tok0 + n + HAL)
                    c0 = lo - (tok0 - HAL)
                    wkeys = [kui]
                    if c0 > 0:
                        memset("dve", ui[:, :, 0:c0], 0.0, [kui])
                    if hi < tok0 + n + HAL:
                        memset("dve", ui[:, :, c0 + hi - lo:n + 2 * HAL], 0.0, [kui])
                    dma("sp", ui[:, :, c0:c0 + hi - lo], uc_d[b, :, :, lo:hi].rearrange("c p t -> p c t"), w=[kui])
                    for cj in range(2):
                        for k in range(CONVK):
                            mm(bank(cj)[:, 0:n], dg[:, cj * CONVK + k, :], ui[:, cj, k:k + n], k == 0, k == CONVK - 1,
                               [kui, "dgc"], [bk(cj)])
                        act(yc[cj][:, 0:n], bank(cj)[:, 0:n], AF.Identity, [bk(cj), "cvec"], [("yc", cj)],
                            bias=cvec[:, 0, cj:cj + 1])
                    for cj in range(2):
                        mm(bank(2)[:, 0:n], ones256[:], yc[cj][:, 0:n], cj == 0, cj == 1, [("yc", cj)], [bk(2)])
                    for cj in range(2):
                        act(ysq[:, 0:n], yc[cj][:, 0:n], AF.Square, [("yc", cj)], ["ysq"])
                        mm(bank(3)[:, 0:n], ones256[:], ysq[:, 0:n], cj == 0, cj == 1, ["ysq"], [bk(3)])
                    act(m2[:, 0:n], bank(2)[:, 0:n], AF.Square, [bk(2)], ["m2"])
                    tt("dve", rstdc[:, 0:n], bank(3)[:, 0:n], m2[:, 0:n], ALU.subtract, [bk(3), "m2"], ["rstdc"])
                    rsqrt_act(rstdc[:, 0:n], rstdc[:, 0:n], ["rstdc"], ["rstdc"])
                    yo = ycb[si % 2]
                    kyo = ("ycb", si % 2)
                    for cj in range(2):
                        tt("dve", tmpc[:, 0:n], yc[cj][:, 0:n], bank(2)[:, 0:n], ALU.subtract, [("yc", cj), bk(2)], ["tmpc"])
                        tt("dve", tmpc[:, 0:n], tmpc[:, 0:n], rstdc[:, 0:n], ALU.mult, ["tmpc", "rstdc"], ["tmpc"])
                        act(yo[:, cj, 0:n], tmpc[:, 0:n], AF.Silu, ["tmpc", "cvec"], [kyo],
                            scale=cvec[:, 1, cj:cj + 1], bias=cvec[:, 2, cj:cj + 1])
                    dma("pool", ycat_d[b, 6:8, :, tok0:tok0 + n].rearrange("c p t -> p c t"), yo[:, :, 0:n], r=[kyo])
                barrier()

                ar.off = p1_mark
                Vt = ar.alloc([NT, 512], BF16)
                for t0 in range(0, NT, 8):
                    t1 = min(NT, t0 + 8)
                    dma("sp", Vt[:, t0:t1, :], vt_d[b, t0:t1, :, :].rearrange("t p c -> p t c"), w=[("Vt", t0)])
                qin = [ar.alloc([512], BF16) for _ in range(2)]
                Et = [ar.alloc([2, 512], BF16) for _ in range(3)]
                rinv = ar.alloc([2, 512], F32)
                accs = [ar.alloc([2, 512], F32) for _ in range(2)]
                o1 = ar.alloc([512], F32)
                o2 = ar.alloc([512], F32)
                osq = ar.alloc([512], F32)
                rstda = ar.alloc([512], F32)
                ybo = [ar.alloc([512], BF16) for _ in range(2)]
                ui_ = 0
                ei = 0
                for si, (tok0, n, is_ctx, seg0, seglen) in enumerate(seg_tiles(b)):
                    if last and is_ctx:
                        continue
                    ktiles = list(range(NTL, NT)) if is_ctx else list(range(NT))
                    for h in range(4):
                        qi = qin[ui_ % 2]
                        kqi = ("qin", ui_ % 2)
                        yo = ybo[ui_ % 2]
                        kyo = ("ybo", ui_ % 2)
                        sb0 = 4 * 0
                        ui_ += 1
                        dma("sp", qi[:, 0:n], qt_d[b, h, :, tok0:tok0 + n], w=[kqi])
                        items = []
                        for kti, kt in enumerate(ktiles):
                            items.append((kti, kt, (ei % 2) * 2, Et[ei % 3], ("E", ei % 3)))
                            ei += 1

                        def emit_S(item, qi=qi, kqi=kqi, h=h, n=n):
                            kti, kt, sp_, E, kE = item
                            for m in range(2):
                                mm(bank(sp_ + m)[:, 0:n], KT[m * 64:(m + 1) * 64, h, kt * 128:(kt + 1) * 128],
                                   qi[m * 64:(m + 1) * 64, 0:n], True, True, [kqi], [bk(sp_), bk(sp_ + 1)])
                            act(E[:, :, 0:n], pp[:, sp_:sp_ + 2, 0:n], AF.Exp, [bk(sp_), bk(sp_ + 1)], [kE])

                        def emit_PV(item, h=h, n=n, nk=len(ktiles)):
                            kti, kt, sp_, E, kE = item
                            first = kti == 0
                            lastk = kti == nk - 1
                            for m in range(2):
                                mm(bank(4 + m)[:, 0:n], Vt[:, kt, h * 128:(h + 1) * 128], E[:, m, 0:n], first, lastk,
                                   [kE, ("Vt", (kt // 8) * 8)], [bk(4 + m)])
                            ac = accs[kti % 2]
                            ka = ("acc", kti % 2, 0)
                            if kti < 2:
                                cp("dve", ac[:, 0, 0:n], E[:, 0, 0:n], [kE], [ka])
                            else:
                                tt("dve", ac[:, 0, 0:n], ac[:, 0, 0:n], E[:, 0, 0:n], ALU.add, [kE, ka], [ka])
                            mm(bank(7)[:, 0:n], onesb[:], E[:, 1, 0:n], first, lastk, [kE], [bk(7)])

                        emit_S(items[0])
                        for ii in range(len(items)):
                            if ii + 1 < len(items):
                                emit_S(items[ii + 1])
                            emit_PV(items[ii])
                        sl = items[-1][2]
                        npar = min(2, len(items))
                        for m in range(1):
                            for pa in range(npar):
                                mm(bank(6 + m)[:, 0:n], onesf[:], accs[pa][:, m, 0:n], pa == 0, pa == npar - 1,
                                   [("acc", pa, m)], [bk(6 + m)])
                        S.add("dve", lambda e, n=n, rinv=rinv: e.reciprocal(out=rinv[:, :, 0:n], in_=pp[:, 6:8, 0:n]),
                              [bk(6), bk(7)], ["rinv"])
                        tt("dve", o1[:, 0:n], bank(4)[:, 0:n], rinv[:, 0, 0:n], ALU.mult, [bk(4), "rinv"], ["o1"])
                        tt("dve", o2[:, 0:n], bank(5)[:, 0:n], rinv[:, 1, 0:n], ALU.mult, [bk(5), "rinv"], ["o2"])
                        stt("dve", o1[:, 0:n], o2[:, 0:n], neglam[:, 0:1], o1[:, 0:n], ALU.mult, ALU.add,
                            ["o1", "o2", "neglam"], ["o1"])
                        act(osq[:, 0:n], o1[:, 0:n], AF.Square, ["o1"], ["osq"])
                        mm(bank(sl)[:, 0:n], ones128[:], osq[:, 0:n], True, True, ["osq"], [bk(sl)])
                        rsqrt_act(rstda[:, 0:n], bank(sl)[:, 0:n], [bk(sl)], ["rstda"])
                        tt("dve", o1[:, 0:n], o1[:, 0:n], rstda[:, 0:n], ALU.mult, ["o1", "rstda"], ["o1"])
                        ts("dve", yo[:, 0:n], o1[:, 0:n], subg[:, 0:1], ALU.mult, ["o1", "subg"], [kyo])
                        dma("pool", ycat_d[b, 2 + h, :, tok0:tok0 + n], yo[:, 0:n], r=[kyo])
                barrier()

            ar = Arena(arena_t, arena_kb * 256)
            w_out = ar.alloc([KC, D], BF16)
            for k in range(KC):
                dma("pool", w_out[:, k, :], w_out_d[l, k * 128:(k + 1) * 128, :], w=[("w_out", k)])
            dgs = [ar.alloc([128], F32) for _ in range(2)]
            g1rep = [ar.alloc([D], F32) for _ in range(2)]
            yct = [ar.alloc([8, 512], BF16) for _ in range(2)]
            xts = [ar.alloc([D], F32) for _ in range(3)]
            sq = ar.alloc([D], F32)
            ms = ar.alloc([2], F32)
            xn = ar.alloc([D], BF16)
            h2s = [ar.alloc([KC, 512], BF16) for _ in range(2)]
            for b in range(NB):
                gvec_rep(g1rep[0], 16, b, 0)
                if not last:
                    gvec_rep(g1rep[1], 16, NB, 0)
                xi = 0
                for si, (tok0, n, is_ctx, seg0, seglen) in enumerate(seg_tiles(b)):
                    if last and is_ctx:
                        continue
                    s = stream_of(b, is_ctx)
                    gr = g1rep[1 if is_ctx else 0]
                    yt = yct[si % 2]
                    kyt = ("yct", si % 2)
                    h2 = h2s[si % 2]
                    kh2 = ("h2s", si % 2)
                    dma("sp", yt[:, :, 0:n], ycat_d[b, :, :, tok0:tok0 + n].rearrange("c p t -> p c t"), w=[kyt])
                    for i in range(n // 128):
                        xt = xts[xi % 3]
                        kx = ("xt", xi % 3)
                        xi += 1
                        dma("sp", xt, x_src(l, b, tok0 + i * 128, 128), w=[kx])
                        for half in range(2):
                            for k in range(KC):
                                mm(bank(1 + half), yt[:, k, i * 128:(i + 1) * 128], w_out[:, k, half * 512:(half + 1) * 512],
                                   k == 0, k == KC - 1, [kyt, ("w_out", k)], [bk(1 + half)])
                        for half in range(2):
                            hs = slice(half * 512, (half + 1) * 512)
                            tt("dve", sq[:, hs], bank(1 + half), gr[:, hs], ALU.mult, [bk(1 + half), "grep"], ["sq"])
                            tt("dve", xt[:, hs], xt[:, hs], sq[:, hs], ALU.add, [kx, "sq"], [kx])
                        dma("pool", xs_d[b, tok0 + i * 128:tok0 + (i + 1) * 128, :], xt, r=[kx])
                        norm_modT(xt, sq, ms[:, 0:1], ms[:, 1:2], xn, h2[:, :, i * 128:(i + 1) * 128],
                                  A2[:, :, s], modT[:, 24:32, s], 3, kx, "ms", "xn", kh2)
                    dma("pool", h2T_d[b, :, :, tok0:tok0 + n].rearrange("c p t -> p c t"), h2[:, :, 0:n], r=[kh2])
            barrier()

            NJS = NJ // 2
            for sweep in range(2):
                ar = Arena(arena_t, arena_kb * 256)
                J0 = sweep * NJS
                wg = ar.alloc([KC, NJS * 128], BF16)
                wv = ar.alloc([KC, NJS * 128], BF16)
                wd = ar.alloc([NJS, D], BF16)
                for k in range(KC):
                    dma("pool", wg[:, k, :], w_gate_d[l, k * 128:(k + 1) * 128, J0 * 128:(J0 + NJS) * 128], w=[("wg", k)])
                    dma("pool", wv[:, k, :], w_val_d[l, k * 128:(k + 1) * 128, J0 * 128:(J0 + NJS) * 128], w=[("wv", k)])
                for j in range(NJS):
                    dma("pool", wd[:, j, :], w_down_d[l, (J0 + j) * 128:(J0 + j + 1) * 128, :], w=[("wd", j)])
                dgs = [ar.alloc([128], F32) for _ in range(2)]
                g2rep = [ar.alloc([D], F32) for _ in range(2)]
                h2in = [ar.alloc([KC, 514], BF16) for _ in range(2)]
                gts = [ar.alloc([514], F32) for _ in range(2)]
                acc = [ar.alloc([512], F32) for _ in range(2)]
                sg = [ar.alloc([512], F32) for _ in range(2)]
                aT = ar.alloc([NJS, 512], BF16)
                xts = [ar.alloc([D], F32) for _ in range(3)]
                tmp = ar.alloc([512], F32)
                for b in range(NB):
                    gvec_rep(g2rep[0], 40, b, 7)
                    if not last:
                        gvec_rep(g2rep[1], 40, NB, 7)
                    xi = 0
                    ji = 0
                    for si, (tok0, n, is_ctx, seg0, seglen) in enumerate(seg_tiles(b)):
                        if last and is_ctx:
                            continue
                        gr = g2rep[1 if is_ctx else 0]
                        hi_ = h2in[si % 2]
                        khi = ("h2in", si % 2)
                        lo = max(seg0, tok0 - 1)
                        hi = min(seg0 + seglen, tok0 + n + 1)
                        c0 = lo - (tok0 - 1)
                        if c0 > 0:
                            memset("dve", hi_[:, :, 0:c0], 0.0, [khi])
                        if hi < tok0 + n + 1:
                            memset("dve", hi_[:, :, n + 1:n + 2], 0.0, [khi])
                        dma("sp", hi_[:, :, c0:c0 + hi - lo], h2T_d[b, :, :, lo:hi].rearrange("c p t -> p c t"), w=[khi])
                        for j in range(NJS):
                            gb = ji % 2
                            g_ = gts[ji % 2]
                            a_ = acc[ji % 2]
                            s_ = sg[ji % 2]
                            kk_ = ji % 2
                            ji += 1
                            for k in range(KC):
                                mm(bank(gb)[:, 0:n], wg[:, k, j * 128:(j + 1) * 128], hi_[:, k, 1:n + 1],
                                   k == 0, k == KC - 1, [khi, ("wg", k)], [bk(gb)])
                            for k in range(KC):
                                mm(bank(4)[:, 0:2], wg[:, k, j * 128:(j + 1) * 128],
                                   hi_[:, k, 0:n + 2:n + 1], k == 0, k == KC - 1, [khi, ("wg", k)], [bk(4)])
                            for k in range(KC):
                                mm(bank(2 + gb)[:, 0:n], wv[:, k, j * 128:(j + 1) * 128], hi_[:, k, 1:n + 1],
                                   k == 0, k == KC - 1, [khi, ("wv", k)], [bk(2 + gb)])
                            cp("act", g_[:, 1:n + 1], bank(gb)[:, 0:n], [bk(gb)], [("gts", kk_)])
                            cp("act", g_[:, 0:n + 2:n + 1], bank(4)[:, 0:2], [bk(4)], [("gts", kk_)])
                            jj = J0 + j
                            ts("dve", a_[:, 0:n], g_[:, 0:n], fcw[:, jj, 0:1], ALU.mult, [("gts", kk_), "fcw"], [("acc", kk_)])
                            stt("dve", a_[:, 0:n], g_[:, 1:n + 1], fcw[:, jj, 1:2], a_[:, 0:n], ALU.mult, ALU.add,
                                [("gts", kk_), ("acc", kk_), "fcw"], [("acc", kk_)])
                            stt("dve", a_[:, 0:n], g_[:, 2:n + 2], fcw[:, jj, 2:3], a_[:, 0:n], ALU.mult, ALU.add,
                                [("gts", kk_), ("acc", kk_), "fcw"], [("acc", kk_)])
                            act(s_[:, 0:n], a_[:, 0:n], AF.Silu, [("acc", kk_), "fcb"], [("sg", kk_)], bias=fcb[:, jj:jj + 1])
                            tt("dve", aT[:, j, 0:n], s_[:, 0:n], bank(2 + gb)[:, 0:n], ALU.mult, [("sg", kk_), bk(2 + gb)], [("aT", j)])
                        for i in range(n // 128):
                            xt = xts[xi % 3]
                            kx = ("xt", xi % 3)
                            xi += 1
                            dma("sp", xt, xs_d[b, tok0 + i * 128:tok0 + (i + 1) * 128, :], w=[kx])
                            for half in range(2):
                                for j in range(NJS):
                                    mm(bank(5 + half), aT[:, j, i * 128:(i + 1) * 128], wd[:, j, half * 512:(half + 1) * 512],
                                       j == 0, j == NJS - 1, [("aT", j), ("wd", j)], [bk(5 + half)])
                            for half in range(2):
                                hs = slice(half * 512, (half + 1) * 512)
                                tt("dve", tmp, bank(5 + half), gr[:, hs], ALU.mult, [bk(5 + half), "grep"], ["tmp"])
                                tt("dve", xt[:, hs], xt[:, hs], tmp, ALU.add, [kx, "tmp"], [kx])
                            if last and sweep == 1:
                                dst = out_d[b, tok0 + i * 128:tok0 + (i + 1) * 128, :]
                            else:
                                dst = xs_d[b, tok0 + i * 128:tok0 + (i + 1) * 128, :]
                            dma("pool", dst, xt, r=[kx])
                barrier()

        S.emit(nc, st)
    return nc


def _strided2(ap2d, stride):
    return ap2d[:, 0:2 * stride].rearrange("p (a c) -> p a c", a=2)[:, :, 0]


def _rope_tables(L):
    ntl = L // 128
    tok = np.arange(L)
    row = (tok // GRID_W).astype(np.float32)
    col = (tok % GRID_W).astype(np.float32)
    inv = (10000.0 ** (-np.arange(0, 32, 2, dtype=np.float32) / 32)).astype(np.float32)
    ang = np.stack([row[:, None] * inv, col[:, None] * inv], axis=1).astype(np.float32)
    cos = np.cos(ang).astype(np.float32)
    sin = np.sin(ang).astype(np.float32)

    def lay(t):
        t = np.repeat(t[:, :, None, :], 2, axis=2).reshape(L, 64)
        return np.ascontiguousarray(t.reshape(ntl, 128, 64).transpose(1, 0, 2))
    return lay(cos), lay(sin)


def _fm(v, nchunk):
    sh = v.shape[:-1]
    return np.ascontiguousarray(np.swapaxes(v.reshape(*sh, nchunk, 128), -1, -2))


def make_in_maps(inputs, n_cores, NB, L, C):
    f = lambda a: np.ascontiguousarray(np.asarray(a, dtype=np.float32))
    x = f(inputs["x"]); c = f(inputs["c"]); ctx = f(inputs["ctx"]); c_ctx = f(inputs["c_ctx"])
    depth = inputs["w_mod"].shape[0]
    rcos, rsin = _rope_tables(L)
    shared = {
        "w_mod": f(inputs["w_mod"]),
        "b_modT": _fm(f(inputs["b_mod"]), 48),
        "norm1_gT": _fm(f(inputs["norm1_g"]), KC),
        "norm2_gT": _fm(f(inputs["norm2_g"]), KC),
        "w_in": f(inputs["w_in"]),
        "ln_v": np.ascontiguousarray(np.stack([f(inputs["ln_v_g"]), f(inputs["ln_v_b"])], axis=1)),
        "w_sT": np.ascontiguousarray(f(inputs["w_s"]).transpose(0, 1, 3, 2)),
        "b_sT": np.ascontiguousarray(f(inputs["b_s"]).transpose(0, 2, 1)),
        "qk_g": np.ascontiguousarray(np.stack([f(inputs["q_norm_g"]), f(inputs["k_norm_g"])], axis=1)),
        "lam": np.ascontiguousarray(np.stack([f(inputs["lam_q1"]), f(inputs["lam_k1"]),
                                              f(inputs["lam_q2"]), f(inputs["lam_k2"])], axis=1)),
        "subln_gT": np.ascontiguousarray(f(inputs["subln_g"])[:, :, None]),
        "conv_wT": np.ascontiguousarray(f(inputs["conv_w"]).reshape(depth, CONVK, 2, 128).transpose(0, 3, 2, 1)),
        "c_vecT": np.ascontiguousarray(np.stack([f(inputs["conv_b"]), f(inputs["ln_c_g"]), f(inputs["ln_c_b"])],
                                                axis=1).reshape(depth, 3, 2, 128).transpose(0, 3, 1, 2)),
        "w_out": f(inputs["w_out"]),
        "w_gate": f(inputs["w_gate"]),
        "w_val": f(inputs["w_val"]),
        "ffn_conv_wT": np.ascontiguousarray(f(inputs["ffn_conv_w"]).reshape(depth, 3, NJ, 128).transpose(0, 3, 2, 1)),
        "ffn_conv_bT": _fm(f(inputs["ffn_conv_b"]), NJ),
        "w_down": f(inputs["w_down"]),
        "rope_cos": rcos,
        "rope_sin": rsin,
    }
    maps = []
    for i in range(n_cores):
        bs = slice(i * NB, (i + 1) * NB)
        cv = np.concatenate([c[bs], c_ctx[None, :]], axis=0)
        cT = np.ascontiguousarray(cv.reshape(NB + 1, KC, 128).transpose(2, 1, 0))
        m = dict(shared)
        m["x"] = np.ascontiguousarray(x[bs])
        m["ctx"] = np.ascontiguousarray(ctx[bs])
        m["cT"] = cT
        maps.append(m)
    return maps


_NC_CACHE = {}


def kernel(**inputs):
    x = inputs["x"]
    B, L, _ = x.shape
    C = inputs["ctx"].shape[1]
    depth = inputs["w_mod"].shape[0]
    NB = B // N_CORES
    key = (L, C, NB, depth)
    if key not in _NC_CACHE:
        _NC_CACHE[key] = build_program(L, C, NB, depth)
    nc = _NC_CACHE[key]
    in_maps = make_in_maps(inputs, N_CORES, NB, L, C)
    res = run_bass_kernel_spmd(nc, in_maps, core_ids=list(range(N_CORES)))
    out = np.concatenate([np.asarray(r["out"], dtype=np.float32) for r in res.results], axis=0)
    return out.reshape(B, L, D)
```

```python
import contextlib
import math
import numpy as np
import concourse.bass as bass
import concourse.mybir as mybir
from concourse.bass_utils import run_bass_kernel_spmd

F32 = mybir.dt.float32
BF16 = mybir.dt.bfloat16
AF = mybir.ActivationFunctionType
ALU = mybir.AluOpType
AX = mybir.AxisListType

D = 1024
KC = 8
GRID_W = 64
EPS = 1e-6
IN_COLS = 2560
COL_Q, COL_K, COL_V, COL_C = 512, 1024, 1536, 2048
DFF = 2816
NJ = 22
CONVK = 31
N_CORES = 8

ENGS = ("pe", "act", "dve", "pool", "sp")
EPOCH = 20000
N_DMA_SEMS = 24
ROPE_ENG = "pool"


class Op:
    __slots__ = ("eng", "fn", "deps", "signal", "sig", "dma", "idx", "prev_dma")

    def __init__(self, eng, fn, dma, idx):
        self.eng = eng
        self.fn = fn
        self.dma = dma
        self.deps = []
        self.signal = dma
        self.sig = None
        self.idx = idx
        self.prev_dma = None


class Sched:
    def __init__(self):
        self.ops = []
        self.lastw = {}
        self.readers = {}
        self.bar = None
        self.last_eng = {}
        self.dma_rr = {e: 0 for e in ENGS}
        self.dma_cnt = {}
        self.dma_last = {}

    def _link(self, op, d):
        if d is op:
            return
        if d.dma:
            op.deps.append(d)
        elif d.eng != op.eng:
            d.signal = True
            op.deps.append(d)
        elif op.dma or op.eng != "pe":
            d.signal = True
            op.deps.append(d)

    def add(self, eng, fn, r=(), w=(), dma=False):
        op = Op(eng, fn, dma, len(self.ops))
        deps = {}
        if self.bar is not None:
            deps[self.bar.idx] = self.bar
        for k in r:
            d = self.lastw.get(k)
            if d is not None:
                deps[d.idx] = d
        for k in w:
            d = self.lastw.get(k)
            if d is not None:
                deps[d.idx] = d
            for rd in self.readers.get(k, ()):
                deps[rd.idx] = rd
        for d in deps.values():
            self._link(op, d)
        for k in r:
            self.readers.setdefault(k, []).append(op)
        for k in w:
            self.lastw[k] = op
            self.readers[k] = []
        if dma:
            k = self.dma_rr[eng] % N_DMA_SEMS
            self.dma_rr[eng] += 1
            key = ("dma", eng, k)
            self.dma_cnt[key] = self.dma_cnt.get(key, 0) + 16
            op.sig = (key, self.dma_cnt[key])
            op.prev_dma = self.dma_last.get(key)
            self.dma_last[key] = op
        else:
            self.last_eng[eng] = op
        self.ops.append(op)
        return op

    def barrier(self, fn):
        op = Op("dve", fn, False, len(self.ops))
        for e, d in self.last_eng.items():
            self._link(op, d)
        for d in self.dma_last.values():
            op.deps.append(d)
        self.last_eng["dve"] = op
        self.ops.append(op)
        self.bar = op
        self.lastw = {}
        self.readers = {}
        return op

    def emit(self, nc, stack):
        cnt = {e: 0 for e in ENGS}
        n_epochs = {e: 1 for e in ENGS}
        for op in self.ops:
            if (not op.dma) and op.signal:
                cnt[op.eng] += 1
                ep = (cnt[op.eng] - 1) // EPOCH
                n_epochs[op.eng] = ep + 1
                op.sig = (("c", op.eng, ep), cnt[op.eng] - ep * EPOCH)
        sems = {}

        def get_sem(key):
            if key not in sems:
                sems[key] = stack.enter_context(nc.semaphore("s_" + "_".join(str(x) for x in key)))
            return sems[key]

        for e in ENGS:
            for ep in range(n_epochs[e]):
                get_sem(("c", e, ep))
        for key in self.dma_cnt:
            get_sem(key)
        per_eng = {e: [op for op in self.ops if op.eng == e] for e in ENGS}
        dma_last = self.dma_last

        def run_engine(ename, eng):
            waited = {}
            for op in per_eng[ename]:
                need = {}
                for d in op.deps:
                    key, val = d.sig
                    if waited.get(key, 0) < val and need.get(key, 0) < val:
                        need[key] = val
                if op.dma and op.prev_dma is not None:
                    key, val = op.prev_dma.sig
                    if waited.get(key, 0) < val and need.get(key, 0) < val:
                        need[key] = val
                for key, val in need.items():
                    eng.wait_ge(get_sem(key), val)
                    waited[key] = val
                inst = op.fn(eng)
                if op.dma:
                    inst.then_inc(get_sem(op.sig[0]), 16)
                elif op.signal:
                    inst.then_inc(get_sem(op.sig[0]), 1)
            for key, op in dma_last.items():
                if key[1] == ename:
                    k, val = op.sig
                    if waited.get(k, 0) < val:
                        eng.wait_ge(get_sem(k), val)

        block = stack.enter_context(nc.Block())

        @block.tensor
        def _(e):
            run_engine("pe", e)

        @block.scalar
        def _(e):
            run_engine("act", e)

        @block.vector
        def _(e):
            run_engine("dve", e)

        @block.gpsimd
        def _(e):
            run_engine("pool", e)

        @block.sync
        def _(e):
            run_engine("sp", e)


class Arena:
    def __init__(self, t, nwords):
        self.t = t
        self.n = nwords
        self.off = 0

    def alloc(self, free_shape, dt):
        esz = 2 if dt == BF16 else 4
        nel = 1
        for s in free_shape:
            nel *= s
        nw = (nel * esz + 3) // 4
        nw = (nw + 7) // 8 * 8
        assert self.off + nw <= self.n, ("arena overflow", self.off, nw, self.n)
        v = self.t[:, self.off:self.off + nw]
        self.off += nw
        if dt == BF16:
            v = v.bitcast(BF16)
        v = v[:, 0:nel]
        if len(free_shape) == 2:
            v = v.rearrange("p (a b) -> p a b", a=free_shape[0])
        elif len(free_shape) == 3:
            v = v.rearrange("p (a b c) -> p a b c", a=free_shape[0], b=free_shape[1])
        return v


def build_program(L, C, NB, DEPTH, arena_kb=160):
    T = L + C
    NT = T // 128
    NTL = L // 128
    nc = bass.Bass("TRN2", target_bir_lowering=False)
    S = Sched()

    def din(name, shape, dt=F32):
        return nc.dram_tensor(name, list(shape), dt, kind="ExternalInput").ap()

    x_in = din("x", [NB, L, D])
    ctx_in = din("ctx", [NB, C, D])
    cT_d = din("cT", [128, KC, NB + 1])
    w_mod_d = din("w_mod", [DEPTH, D, 6 * D])
    b_modT_d = din("b_modT", [DEPTH, 128, 48])
    n1g_d = din("norm1_gT", [DEPTH, 128, KC])
    n2g_d = din("norm2_gT", [DEPTH, 128, KC])
    w_in_d = din("w_in", [DEPTH, D, IN_COLS])
    lnv_d = din("ln_v", [DEPTH, 2, 256])
    w_sT_d = din("w_sT", [DEPTH, 4, 128, 128])
    b_sT_d = din("b_sT", [DEPTH, 128, 4])
    qkg_d = din("qk_g", [DEPTH, 2, 64])
    lam_d = din("lam", [DEPTH, 4, 64])
    subg_d = din("subln_gT", [DEPTH, 128, 1])
    convw_d = din("conv_wT", [DEPTH, 128, 2, CONVK])
    cvec_d = din("c_vecT", [DEPTH, 128, 3, 2])
    w_out_d = din("w_out", [DEPTH, D, D])
    w_gate_d = din("w_gate", [DEPTH, D, DFF])
    w_val_d = din("w_val", [DEPTH, D, DFF])
    fcw_d = din("ffn_conv_wT", [DEPTH, 128, NJ, 3])
    fcb_d = din("ffn_conv_bT", [DEPTH, 128, NJ])
    w_down_d = din("w_down", [DEPTH, DFF, D])
    rcos_d = din("rope_cos", [128, NTL, 64])
    rsin_d = din("rope_sin", [128, NTL, 64])
    out_d = nc.dram_tensor("out", [NB, L, D], F32, kind="ExternalOutput").ap()

    xs_d = nc.dram_tensor("xs", [NB, T, D], F32).ap()
    ycat_d = nc.dram_tensor("ycat", [NB, 8, 128, T], BF16).ap()
    h2T_d = nc.dram_tensor("h2T", [NB, 8, 128, T], BF16).ap()
    uc_d = nc.dram_tensor("uc", [NB, 2, 128, T], BF16).ap()
    qt_d = nc.dram_tensor("qt", [NB, 4, 128, T], BF16).ap()
    vt_d = nc.dram_tensor("vt", [NB, NT, 128, 512], BF16).ap()

    st = contextlib.ExitStack()
    with st:
        def sb(name, shape, dt=F32):
            return st.enter_context(nc.sbuf_tensor(name, list(shape), dt))

        identf = sb("identf", [128, 128])
        identb = sb("identb", [128, 128], BF16)
        iot = sb("iot", [128, 128])
        onesb = sb("onesb", [128, 128], BF16)
        onesf = sb("onesf", [128, 128])
        ones256 = sb("ones256", [128, 128])
        ones128 = sb("ones128", [128, 128])
        barsc = sb("barsc", [128, 1])
        rcos = sb("rcos", [128, NTL, 64])
        rsin = sb("rsin", [128, NTL, 64])
        cTs = sb("cTs", [128, KC, NB + 1])
        siluT = sb("siluT", [128, KC, NB + 1])
        modT = sb("modT", [128, 48, NB + 1])
        bmod = sb("bmod", [128, 48])
        n1g = sb("n1g", [128, KC])
        n2g = sb("n2g", [128, KC])
        A1 = sb("A1", [128, KC, NB + 1])
        A2 = sb("A2", [128, KC, NB + 1])
        lnvg = sb("lnvg", [128, 256])
        lnvb = sb("lnvb", [128, 256])
        bsT = sb("bsT", [128, 4])
        gq = sb("gq", [128, 64])
        gk = sb("gk", [128, 64])
        lamv = sb("lamv", [128, 4, 64])
        lamp = sb("lamp", [128, 2, 64])
        lams = sb("lams", [128, 2])
        neglam = sb("neglam", [128, 1])
        subg = sb("subg", [128, 1])
        convw = sb("convw", [128, 2, CONVK])
        cvec = sb("cvec", [128, 3, 2])
        fcw = sb("fcw", [128, NJ, 3])
        fcb = sb("fcb", [128, NJ])
        wsT = sb("wsT", [128, 4, 128], BF16)
        wsTf = sb("wsTf", [128, 4, 128])
        arena_t = sb("arena", [128, arena_kb * 256])
        pp = st.enter_context(nc.psum_tensor("pp", [128, 8, 512], F32))

        def bank(i):
            return pp[:, i, :]

        def bankb(i):
            return pp[:, i, :].bitcast(BF16)

        def bk(i):
            return ("bank", i)

        def mm(out, lhsT, rhs, start, stop, r, w):
            S.add("pe", lambda e: e.matmul(out, lhsT=lhsT, rhs=rhs, start=start, stop=stop), r, w)

        def tr(out, in_, r, w):
            S.add("pe", lambda e: e.transpose(out=out, in_=in_, identity=identb[:]), r, w)

        def act(out, in_, func, r, w, scale=1.0, bias=0.0, accum=None):
            if accum is None:
                S.add("act", lambda e: e.activation(out=out, in_=in_, func=func, bias=bias, scale=scale), r, w)
            else:
                S.add("act", lambda e: e.activation(out=out, in_=in_, func=func, bias=bias, scale=scale,
                                                    accum_out=accum), r, w)

        def tt(eng, out, in0, in1, op, r, w):
            S.add(eng, lambda e: e.tensor_tensor(out=out, in0=in0, in1=in1, op=op), r, w)

        def ts(eng, out, in0, s1, op0, r, w, s2=None, op1=None):
            if op1 is None:
                S.add(eng, lambda e: e.tensor_scalar(out=out, in0=in0, scalar1=s1, scalar2=None, op0=op0), r, w)
            else:
                S.add(eng, lambda e: e.tensor_scalar(out=out, in0=in0, scalar1=s1, scalar2=s2, op0=op0, op1=op1), r, w)

        def stt(eng, out, in0, scalar, in1, op0, op1, r, w):
            S.add(eng, lambda e: e.scalar_tensor_tensor(out=out, in0=in0, scalar=scalar, in1=in1, op0=op0, op1=op1), r, w)

        def cp(eng, out, in_, r, w):
            if eng == "act":
                S.add("act", lambda e: e.copy(out=out, in_=in_), r, w)
            else:
                S.add(eng, lambda e: e.tensor_copy(out=out, in_=in_), r, w)

        def memset(eng, ap, val, w):
            S.add(eng, lambda e: e.memset(ap, val), (), w)

        def dma(eng, out, in_, r=(), w=()):
            S.dma_op = S.add(eng, lambda e: e.dma_start(out=out, in_=in_), r, w, dma=True)

        def barrier():
            S.barrier(lambda e: e.memset(barsc[:], 0.0))

        def rsqrt_act(out, in_, r, w, scale=1.0, eps=EPS):
            act(out, in_, AF.Ln, r, w, scale=scale, bias=epsb[:, 0:1] if eps == EPS else eps)
            act(out, out, AF.Exp, w, w, scale=-0.5)

        epsb = sb("epsb", [128, 1])

        memset("dve", epsb[:], EPS, ["epsb"])
        S.add("pool", lambda e: e.iota(iot[:], pattern=[[1, 128]], base=0, channel_multiplier=-1,
                                       allow_small_or_imprecise_dtypes=True), (), ["iot"])
        S.add("dve", lambda e: e.tensor_single_scalar(out=identf[:], in_=iot[:], scalar=0.0, op=ALU.is_equal),
              ["iot"], ["identf"])
        cp("dve", identb[:], identf[:], ["identf"], ["identb"])
        memset("dve", onesb[:], 1.0, ["onesb"])
        memset("dve", onesf[:], 1.0, ["onesf"])
        memset("dve", ones256[:], 1.0 / 256, ["ones256"])
        memset("dve", ones128[:], 1.0 / 128, ["ones128"])
        dma("sp", rcos[:], rcos_d, w=["rcos"])
        dma("sp", rsin[:], rsin_d, w=["rsin"])
        dma("sp", cTs[:], cT_d, w=["cTs"])
        act(siluT[:], cTs[:], AF.Silu, ["cTs"], ["siluT"])
        barrier()

        def seg_tiles(b):
            res = []
            for s0 in range(0, L, 512):
                res.append((s0, min(512, L - s0), False, 0, L))
            for s0 in range(0, C, 512):
                res.append((L + s0, min(512, C - s0), True, L, C))
            return res

        def x_src(l, b, tok0, n):
            if l == 0:
                if tok0 < L:
                    return x_in[b, tok0:tok0 + n, :]
                return ctx_in[b, tok0 - L:tok0 - L + n, :]
            return xs_d[b, tok0:tok0 + n, :]

        for l in range(DEPTH):
            last = (l == DEPTH - 1)
            lam_init = 0.8 - 0.6 * math.exp(-0.3 * l)
            dma("sp", bmod[:], b_modT_d[l], w=["bmod"])
            dma("sp", n1g[:], n1g_d[l], w=["n1g"])
            dma("sp", n2g[:], n2g_d[l], w=["n2g"])
            dma("sp", lnvg[:], lnv_d[l, 0:1, :].broadcast_to([128, 256]), w=["lnvg"])
            dma("sp", lnvb[:], lnv_d[l, 1:2, :].broadcast_to([128, 256]), w=["lnvb"])
            dma("sp", bsT[:], b_sT_d[l], w=["bsT"])
            dma("sp", gq[:], qkg_d[l, 0:1, :].broadcast_to([128, 64]), w=["gq"])
            dma("sp", gk[:], qkg_d[l, 1:2, :].broadcast_to([128, 64]), w=["gk"])
            for i in range(4):
                dma("sp", lamv[:, i, :], lam_d[l, i:i + 1, :].broadcast_to([128, 64]), w=[("lamv", i)])
            dma("sp", subg[:], subg_d[l], w=["subg"])
            dma("sp", convw[:], convw_d[l], w=["convw"])
            dma("sp", cvec[:], cvec_d[l], w=["cvec"])
            dma("sp", fcw[:], fcw_d[l], w=["fcw"])
            dma("sp", fcb[:], fcb_d[l], w=["fcb"])
            dma("sp", wsTf[:], w_sT_d[l].rearrange("h q p -> q h p"), w=["wsTf"])
            cp("dve", wsT[:], wsTf[:], ["wsTf"], ["wsT"])
            ts("dve", gq[:], gq[:], 0.125, ALU.mult, ["gq"], ["gq"])
            tt("dve", lamp[:, 0, :], lamv[:, 0, :], lamv[:, 1, :], ALU.mult, [("lamv", 0), ("lamv", 1)], ["lamp0"])
            tt("dve", lamp[:, 1, :], lamv[:, 2, :], lamv[:, 3, :], ALU.mult, [("lamv", 2), ("lamv", 3)], ["lamp1"])
            S.add("dve", lambda e: e.tensor_reduce(out=lams[:], in_=lamp[:], axis=AX.X, op=ALU.add),
                  ["lamp0", "lamp1"], ["lams"])
            act(lams[:], lams[:], AF.Exp, ["lams"], ["lams"])
            tt("dve", neglam[:], lams[:, 1:2], lams[:, 0:1], ALU.subtract, ["lams"], ["neglam"])
            ts("dve", neglam[:], neglam[:], -lam_init, ALU.add, ["neglam"], ["neglam"])
            ts("dve", subg[:], subg[:], 1.0 - lam_init, ALU.mult, ["subg"], ["subg"])

            ar = Arena(arena_t, arena_kb * 256)
            NBLK = 8
            BW = 6 * D // NBLK
            wm = [ar.alloc([KC, BW], F32) for _ in range(2)]
            for blk in range(NBLK):
                wt = wm[blk % 2]
                for k in range(KC):
                    dma("sp", wt[:, k, :], w_mod_d[l, k * 128:(k + 1) * 128, blk * BW:(blk + 1) * BW],
                        w=[("wm", blk % 2, k)])
                pbi = blk % 2
                nj = BW // 128
                for jj in range(nj):
                    for k in range(KC):
                        mm(bank(pbi)[:, jj * 4:jj * 4 + NB + 1], wt[:, k, jj * 128:(jj + 1) * 128], siluT[:, k, :],
                           k == 0, k == KC - 1, [("wm", blk % 2, k), "siluT"], [bk(pbi)])
                j0 = blk * nj
                tt("dve", modT[:, j0:j0 + nj, :],
                   bank(pbi)[:, 0:nj * 4].rearrange("p (j c) -> p j c", c=4)[:, :, 0:NB + 1],
                   bmod[:, j0:j0 + nj].unsqueeze(2).broadcast_to([128, nj, NB + 1]), ALU.add,
                   [bk(pbi), "bmod"], ["modT"])
            stt("dve", A1[:], modT[:, 8:16, :], 1.0, n1g[:].unsqueeze(2).broadcast_to([128, KC, NB + 1]),
                ALU.add, ALU.mult, ["modT", "n1g"], ["A1"])
            stt("dve", A2[:], modT[:, 32:40, :], 1.0, n2g[:].unsqueeze(2).broadcast_to([128, KC, NB + 1]),
                ALU.add, ALU.mult, ["modT", "n2g"], ["A2"])
            barrier()

            def stream_of(b, is_ctx):
                return NB if is_ctx else b

            def norm_modT(xt, sq, ms, rstd, xn, hT_out, Acol, Bcol, pbank, kx, kms, kxn, khT):
                act(sq, xt, AF.Square, [kx], ["sq", kms], scale=1.0 / 32, accum=ms)
                rsqrt_act(rstd, ms, [kms], [kms])
                ts("dve", xn, xt, rstd, ALU.mult, [kx, kms], [kxn])
                pb = bankb(pbank).rearrange("p (k n) -> p k n", k=KC)
                for k in range(KC):
                    tr(pb[:, k, :], xn[:, k * 128:(k + 1) * 128], [kxn], [bk(pbank)])
                for k in range(KC):
                    act(hT_out[:, k, :], pb[:, k, :], AF.Identity, [bk(pbank), "A", "modT"], [khT],
                        scale=Acol[:, k:k + 1], bias=Bcol[:, k:k + 1])

            def gvec_rep(dst, jbase, s, pbank):
                for half in range(2):
                    for kk in range(4):
                        k = half * 4 + kk
                        dg = dgs[k % 2]
                        ts("dve", dg, identf[:], modT[:, jbase + k, s:s + 1], ALU.mult, [], [("dg", k % 2)])
                        mm(bank(pbank)[:, kk * 128:(kk + 1) * 128], onesf[:], dg, True, True,
                           [("dg", k % 2)], [bk(pbank)])
                    cp("dve", dst[:, half * 512:(half + 1) * 512], bank(pbank), [bk(pbank)], ["grep"])

            for b in range(NB):
                ar = Arena(arena_t, arena_kb * 256)
                KT = ar.alloc([4, T], BF16)
                w_in = ar.alloc([KC, IN_COLS], BF16)
                if b == 0:
                    for k in range(KC):
                        dma("pool", w_in[:, k, :], w_in_d[l, k * 128:(k + 1) * 128, :], w=[("w_in", k)])
                p1_mark = ar.off
                xts = [ar.alloc([D], F32) for _ in range(2)]
                sq = ar.alloc([D], F32)
                ms = ar.alloc([2], F32)
                xn = ar.alloc([D], BF16)
                hTs = [ar.alloc([KC, 512], BF16) for _ in range(2)]
                Vsts = [ar.alloc([4, 512], BF16) for _ in range(2)]
                sig = ar.alloc([512], F32)
                ucs = [ar.alloc([2, 512], BF16) for _ in range(2)]
                zts = [ar.alloc([512], F32) for _ in range(2)]
                bnst = ar.alloc([8], F32)
                vn = ar.alloc([256], F32)
                vnbs = [ar.alloc([256], BF16) for _ in range(2)]
                yas = [ar.alloc([256], BF16) for _ in range(2)]
                yaTs = [ar.alloc([2, 512], BF16) for _ in range(2)]
                qsqs = [ar.alloc([512], F32) for _ in range(2)]
                stt_ = ar.alloc([24], F32)
                qns = [ar.alloc([512], F32) for _ in range(2)]
                qcss = [ar.alloc([1024], F32) for _ in range(2)]
                qrbs = [[ar.alloc([512], BF16) for _ in range(2)] for _ in range(2)]
                QTs = [ar.alloc([4, 512], BF16) for _ in range(2)]
                wk = [("w_in", k) for k in range(KC)]
                tiles = seg_tiles(b)
                xi_box = [0]

                def emit_norm(si, i):
                    tok0, n, is_ctx, seg0, seglen = tiles[si]
                    s = stream_of(b, is_ctx)
                    xt = xts[xi_box[0] % 2]
                    kx = ("xt", xi_box[0] % 2)
                    xi_box[0] += 1
                    dma("sp", xt, x_src(l, b, tok0 + i * 128, 128), w=[kx])
                    norm_modT(xt, sq, ms[:, 0:1], ms[:, 1:2], xn, hTs[si % 2][:, :, i * 128:(i + 1) * 128],
                              A1[:, :, s], modT[:, 0:8, s], 0, kx, "ms", "xn", ("hT", si % 2))

                def emit_blocks(si, i):
                    tok0, n, is_ctx, seg0, seglen = tiles[si]
                    hT = hTs[si % 2]
                    khT = ("hT", si % 2)
                    ti = (tok0 + i * 128) // 128
                    par = i % 2
                    zt = zts[par]
                    vnb = vnbs[par]
                    hasq = not (last and is_ctx)
                    lhs = [hT[:, k, i * 128:(i + 1) * 128] for k in range(KC)]
                    chains = [(1, COL_K, gk, 1, 0)] + ([(0, COL_Q, gq, 6, 1)] if hasq else [])
                    nst = 2 + 8 * len(chains)
                    v3 = lambda a: a.rearrange("p (g d) -> p g d", g=8)
                    v5 = lambda a: a.rearrange("p (g a h f) -> p g a h f", g=8, a=2, h=2)
                    for k in range(KC):
                        mm(bank(3), lhs[k], w_in[:, k, 0:512], k == 0, k == KC - 1, [khT, wk[k]], [bk(3)])
                    for which, col, g, pbi, sc in chains:
                        for k in range(KC):
                            mm(bank(pbi), lhs[k], w_in[:, k, col:col + 512], k == 0, k == KC - 1,
                               [khT, wk[k]], [bk(pbi)])
                    for k in range(KC):
                        mm(bank(2), lhs[k], w_in[:, k, COL_V:COL_V + 512], k == 0, k == KC - 1,
                           [khT, wk[k]], [bk(2)])
                    act(zt, bank(3), AF.Gelu_apprx_tanh, [bk(3)], [("zt", par)])
                    for which, col, g, pbi, sc in chains:
                        act(qsqs[sc], bank(pbi), AF.Square, [bk(pbi)], [("qsq", sc)], scale=0.125)
                    cp("act", Vsts[si % 2][:, i, :], bank(2), [bk(2)], [("Vst", si % 2)])
                    S.add("dve", lambda e, bnst=bnst, zt=zt: e.bn_stats(out=bnst[:, 0:6], in_=zt[:, 256:512]),
                          [("zt", par)], ["bnst"])
                    S.add("dve", lambda e, bnst=bnst, stt_=stt_: e.bn_aggr(out=stt_[:, 0:2], in_=bnst[:, 0:6]),
                          ["bnst"], ["st"])
                    for which, col, g, pbi, sc in chains:
                        S.add("dve", lambda e, sc=sc, stt_=stt_: e.tensor_reduce(
                            out=stt_[:, 2 + 8 * sc:10 + 8 * sc], in_=qsqs[sc].rearrange("p (g d) -> p g d", g=8),
                            axis=AX.X, op=ALU.add), [("qsq", sc)], ["st"])
                    rsqrt_act(stt_[:, 1:nst], stt_[:, 1:nst], ["st"], ["st"])
                    ts("dve", vn, zt[:, 256:512], stt_[:, 0:1], ALU.subtract, [("zt", par), "st"], ["vn"],
                       s2=stt_[:, 1:2], op1=ALU.mult)
                    tt("dve", vn, vn, lnvg[:], ALU.mult, ["vn", "lnvg"], ["vn"])
                    tt("dve", vnb, vn, lnvb[:], ALU.add, ["vn", "lnvb"], [("vnb", par)])
                    for which, col, g, pbi, sc in chains:
                        qn_ = qns[sc]
                        qc = qcss[sc][:, 0:512]
                        qs_ = qcss[sc][:, 512:1024]
                        qrb = qrbs[which][par]
                        kqrb = ("qrb", which, par)
                        tt("dve", v3(qn_), v3(bank(pbi)),
                           stt_[:, 2 + 8 * sc:10 + 8 * sc].unsqueeze(2).broadcast_to([128, 8, 64]), ALU.mult,
                           [bk(pbi), "st"], [("qn", sc)])
                        gb = g[:].unsqueeze(1).broadcast_to([128, 8, 64])
                        if is_ctx:
                            tt("dve", v3(qrb), v3(qn_), gb, ALU.mult, [("qn", sc), "g"], [kqrb])
                        else:
                            tt("dve", v3(qn_), v3(qn_), gb, ALU.mult, [("qn", sc), "g"], [("qn", sc)])
                            cb = rcos[:, ti, :].unsqueeze(1).broadcast_to([128, 8, 64])
                            sbb = rsin[:, ti, :].unsqueeze(1).broadcast_to([128, 8, 64])
                            tt(ROPE_ENG, v3(qc), v3(qn_), cb, ALU.mult, [("qn", sc), "rcos"], [("qcs", sc)])
                            tt(ROPE_ENG, v3(qs_), v3(qn_), sbb, ALU.mult, [("qn", sc), "rsin"], [("qcs", sc)])
                            for ax in range(2):
                                tt(ROPE_ENG, v5(qrb)[:, :, ax, 0, :], v5(qc)[:, :, ax, 0, :], v5(qs_)[:, :, ax, 1, :],
                                   ALU.subtract, [("qcs", sc)], [kqrb])
                                tt(ROPE_ENG, v5(qrb)[:, :, ax, 1, :], v5(qc)[:, :, ax, 1, :], v5(qs_)[:, :, ax, 0, :],
                                   ALU.add, [("qcs", sc)], [kqrb])

                def emit_dep(si, i):
                    tok0, n, is_ctx, seg0, seglen = tiles[si]
                    ti = (tok0 + i * 128) // 128
                    par = i % 2
                    zt = zts[par]
                    vnb = vnbs[par]
                    ya = yas[par]
                    yaT = yaTs[si % 2]
                    for h in range(4):
                        mm(bank(4)[:, h * 64:(h + 1) * 64], wsT[:, h, :], vnb[:, h * 64:(h + 1) * 64],
                           True, True, [("vnb", par), "wsT"], [bk(4)])
                    for h in range(4):
                        stt("dve", ya[:, h * 64:(h + 1) * 64], bank(4)[:, h * 64:(h + 1) * 64], bsT[:, h:h + 1],
                            zt[:, h * 64:(h + 1) * 64], ALU.add, ALU.mult, [bk(4), ("zt", par), "bsT"], [("ya", par)])
                    pbT = bankb(5).rearrange("p (k n) -> p k n", k=8)
                    for cj in range(2):
                        tr(pbT[:, cj, :], ya[:, cj * 128:(cj + 1) * 128], [("ya", par)], [bk(5)])
                    cp("dve", yaT[:, :, i * 128:(i + 1) * 128], pbT[:, 0:2, :], [bk(5)], [("yaT", si % 2)])
                    for which in range(2):
                        if which == 0 and last and is_ctx:
                            continue
                        qrb = qrbs[which][par]
                        pbT = bankb(7).rearrange("p (k n) -> p k n", k=8)
                        for h in range(4):
                            tr(pbT[:, h, :], qrb[:, h * 128:(h + 1) * 128], [("qrb", which, par)], [bk(7)])
                        if which == 0:
                            cp("act", QTs[si % 2][:, :, i * 128:(i + 1) * 128], pbT[:, 0:4, :], [bk(7)], [("QTs", si % 2)])
                        else:
                            cp("act", KT[:, :, ti * 128:(ti + 1) * 128], pbT[:, 0:4, :], [bk(7)], [("KT", ti)])

                for i in range(tiles[0][1] // 128):
                    emit_norm(0, i)
                for si, (tok0, n, is_ctx, seg0, seglen) in enumerate(tiles):
                    hT = hTs[si % 2]
                    khT = ("hT", si % 2)
                    nt = n // 128
                    nt_next = tiles[si + 1][1] // 128 if si + 1 < len(tiles) else 0
                    uc = ucs[si % 2]
                    kuc = ("ucs", si % 2)
                    if not (last and is_ctx):
                        for cj in range(2):
                            for part, pbi in ((0, 1), (1, 2)):
                                col = COL_C + part * 256 + cj * 128
                                for k in range(KC):
                                    mm(bank(pbi)[:, 0:n], w_in[:, k, col:col + 128], hT[:, k, 0:n],
                                       k == 0, k == KC - 1, [khT, wk[k]], [bk(pbi)])
                            act(sig[:, 0:n], bank(2)[:, 0:n], AF.Sigmoid, [bk(2)], ["sig"])
                            tt("dve", uc[:, cj, 0:n], bank(1)[:, 0:n], sig[:, 0:n], ALU.mult, [bk(1), "sig"], [kuc])
                        dma("pool", uc_d[b, :, :, tok0:tok0 + n].rearrange("c p t -> p c t"), uc[:, :, 0:n], r=[kuc])
                    for i in range(nt):
                        emit_blocks(si, i)
                        if i > 0:
                            emit_dep(si, i - 1)
                        if i < nt_next:
                            emit_norm(si + 1, i)
                    emit_dep(si, nt - 1)
                    for i in range(nt, nt_next):
                        emit_norm(si + 1, i)
                    dma("pool", ycat_d[b, 0:2, :, tok0:tok0 + n].rearrange("c p t -> p c t"),
                        yaTs[si % 2][:, :, 0:n], r=[("yaT", si % 2)])
                    if not (last and is_ctx):
                        dma("pool", qt_d[b, :, :, tok0:tok0 + n].rearrange("c p t -> p c t"),
                            QTs[si % 2][:, :, 0:n], r=[("QTs", si % 2)])
                    dma("pool", vt_d[b, tok0 // 128:tok0 // 128 + nt, :, :].rearrange("t p c -> p t c"),
                        Vsts[si % 2][:, 0:nt, :], r=[("Vst", si % 2)])
                barrier()

                ar.off = p1_mark
                dg = ar.alloc([2 * CONVK, 128], BF16)
                for cj in range(2):
                    for k in range(CONVK):
                        ts("dve", dg[:, cj * CONVK + k, :], identf[:], convw[:, cj, k:k + 1], ALU.mult, [], ["dgc"])
                HAL = CONVK // 2
                ucin = [ar.alloc([2, 512 + 2 * HAL], BF16) for _ in range(2)]
                yc = [ar.alloc([512], F32) for _ in range(2)]
                ysq = ar.alloc([512], F32)
                m2 = ar.alloc([512], F32)
                rstdc = ar.alloc([512], F32)
                tmpc = ar.alloc([512], F32)
                ycb = [ar.alloc([2, 512], BF16) for _ in range(2)]
                for si, (tok0, n, is_ctx, seg0, seglen) in enumerate(seg_tiles(b)):
                    if last and is_ctx:
                        continue
                    ui = ucin[si % 2]
                    kui = ("ucin", si % 2)
                    lo = max(seg0, tok0 - HAL)
                    hi = min(seg0 + seglen, tok0 + n + HAL)
                    c0 = lo - (tok0 - HAL)
                    wkeys = [kui]
                    if c0 > 0:
                        memset("dve", ui[:, :, 0:c0], 0.0, [kui])
                    if hi < tok0 + n + HAL:
                        memset("dve", ui[:, :, c0 + hi - lo:n + 2 * HAL], 0.0, [kui])
                    dma("sp", ui[:, :, c0:c0 + hi - lo], uc_d[b, :, :, lo:hi].rearrange("c p t -> p c t"), w=[kui])
                    for cj in range(2):
                        for k in range(CONVK):
                            mm(bank(cj)[:, 0:n], dg[:, cj * CONVK + k, :], ui[:, cj, k:k + n], k == 0, k == CONVK - 1,
                               [kui, "dgc"], [bk(cj)])
                        act(yc[cj][:, 0:n], bank(cj)[:, 0:n], AF.Identity, [bk(cj), "cvec"], [("yc", cj)],
                            bias=cvec[:, 0, cj:cj + 1])
                    for cj in range(2):
                        mm(bank(2)[:, 0:n], ones256[:], yc[cj][:, 0:n], cj == 0, cj == 1, [("yc", cj)], [bk(2)])
                    for cj in range(2):
                        act(ysq[:, 0:n], yc[cj][:, 0:n], AF.Square, [("yc", cj)], ["ysq"])
                        mm(bank(3)[:, 0:n], ones256[:], ysq[:, 0:n], cj == 0, cj == 1, ["ysq"], [bk(3)])
                    act(m2[:, 0:n], bank(2)[:, 0:n], AF.Square, [bk(2)], ["m2"])
                    tt("dve", rstdc[:, 0:n], bank(3)[:, 0:n], m2[:, 0:n], ALU.subtract, [bk(3), "m2"], ["rstdc"])
                    rsqrt_act(rstdc[:, 0:n], rstdc[:, 0:n], ["rstdc"], ["rstdc"])
                    yo = ycb[si % 2]
                    kyo = ("ycb", si % 2)
                    for cj in range(2):
                        tt("dve", tmpc[:, 0:n], yc[cj][:, 0:n], bank(2)[:, 0:n], ALU.subtract, [("yc", cj), bk(2)], ["tmpc"])
                        tt("dve", tmpc[:, 0:n], tmpc[:, 0:n], rstdc[:, 0:n], ALU.mult, ["tmpc", "rstdc"], ["tmpc"])
                        act(yo[:, cj, 0:n], tmpc[:, 0:n], AF.Silu, ["tmpc", "cvec"], [kyo],
                            scale=cvec[:, 1, cj:cj + 1], bias=cvec[:, 2, cj:cj + 1])
                    dma("pool", ycat_d[b, 6:8, :, tok0:tok0 + n].rearrange("c p t -> p c t"), yo[:, :, 0:n], r=[kyo])
                barrier()

                ar.off = p1_mark
                Vt = ar.alloc([NT, 512], BF16)
                for t0 in range(0, NT, 8):
                    t1 = min(NT, t0 + 8)
                    dma("sp", Vt[:, t0:t1, :], vt_d[b, t0:t1, :, :].rearrange("t p c -> p t c"), w=[("Vt", t0)])
                qin = [ar.alloc([512], BF16) for _ in range(2)]
                Et = [ar.alloc([2, 512], BF16) for _ in range(3)]
                rinv = ar.alloc([2, 512], F32)
                accs = [ar.alloc([2, 512], F32) for _ in range(2)]
                o1 = ar.alloc([512], F32)
                o2 = ar.alloc([512], F32)
                osq = ar.alloc([512], F32)
                rstda = ar.alloc([512], F32)
                ybo = [ar.alloc([512], BF16) for _ in range(2)]
                ui_ = 0
                ei = 0
                for si, (tok0, n, is_ctx, seg0, seglen) in enumerate(seg_tiles(b)):
                    if last and is_ctx:
                        continue
                    ktiles = list(range(NTL, NT)) if is_ctx else list(range(NT))
                    for h in range(4):
                        qi = qin[ui_ % 2]
                        kqi = ("qin", ui_ % 2)
                        yo = ybo[ui_ % 2]
                        kyo = ("ybo", ui_ % 2)
                        sb0 = 4 * 0
                        ui_ += 1
                        dma("sp", qi[:, 0:n], qt_d[b, h, :, tok0:tok0 + n], w=[kqi])
                        items = []
                        for kti, kt in enumerate(ktiles):
                            items.append((kti, kt, (ei % 2) * 2, Et[ei % 3], ("E", ei % 3)))
                            ei += 1

                        def emit_S(item, qi=qi, kqi=kqi, h=h, n=n):
                            kti, kt, sp_, E, kE = item
                            for m in range(2):
                                mm(bank(sp_ + m)[:, 0:n], KT[m * 64:(m + 1) * 64, h, kt * 128:(kt + 1) * 128],
                                   qi[m * 64:(m + 1) * 64, 0:n], True, True, [kqi], [bk(sp_), bk(sp_ + 1)])
                            act(E[:, :, 0:n], pp[:, sp_:sp_ + 2, 0:n], AF.Exp, [bk(sp_), bk(sp_ + 1)], [kE])

                        def emit_PV(item, h=h, n=n, nk=len(ktiles)):
                            kti, kt, sp_, E, kE = item
                            first = kti == 0
                            lastk = kti == nk - 1
                            for m in range(2):
                                mm(bank(4 + m)[:, 0:n], Vt[:, kt, h * 128:(h + 1) * 128], E[:, m, 0:n], first, lastk,
                                   [kE, ("Vt", (kt // 8) * 8)], [bk(4 + m)])
                            ac = accs[kti % 2]
                            ka = ("acc", kti % 2, 0)
                            if kti < 2:
                                cp("dve", ac[:, 0, 0:n], E[:, 0, 0:n], [kE], [ka])
                            else:
                                tt("dve", ac[:, 0, 0:n], ac[:, 0, 0:n], E[:, 0, 0:n], ALU.add, [kE, ka], [ka])
                            mm(bank(7)[:, 0:n], onesb[:], E[:, 1, 0:n], first, lastk, [kE], [bk(7)])

                        emit_S(items[0])
                        for ii in range(len(items)):
                            if ii + 1 < len(items):
                                emit_S(items[ii + 1])
                            emit_PV(items[ii])
                        sl = items[-1][2]
                        npar = min(2, len(items))
                        for m in range(1):
                            for pa in range(npar):
                                mm(bank(6 + m)[:, 0:n], onesf[:], accs[pa][:, m, 0:n], pa == 0, pa == npar - 1,
                                   [("acc", pa, m)], [bk(6 + m)])
                        act(rinv[:, :, 0:n], pp[:, 6:8, 0:n], AF.Ln, [bk(6), bk(7)], ["rinv"])
                        act(rinv[:, :, 0:n], rinv[:, :, 0:n], AF.Exp, ["rinv"], ["rinv"], scale=-1.0)
                        tt("dve", o1[:, 0:n], bank(4)[:, 0:n], rinv[:, 0, 0:n], ALU.mult, [bk(4), "rinv"], ["o1"])
                        tt("dve", o2[:, 0:n], bank(5)[:, 0:n], rinv[:, 1, 0:n], ALU.mult, [bk(5), "rinv"], ["o2"])
                        stt("dve", o1[:, 0:n], o2[:, 0:n], neglam[:, 0:1], o1[:, 0:n], ALU.mult, ALU.add,
                            ["o1", "o2", "neglam"], ["o1"])
                        act(osq[:, 0:n], o1[:, 0:n], AF.Square, ["o1"], ["osq"])
                        mm(bank(sl)[:, 0:n], ones128[:], osq[:, 0:n], True, True, ["osq"], [bk(sl)])
                        rsqrt_act(rstda[:, 0:n], bank(sl)[:, 0:n], [bk(sl)], ["rstda"])
                        tt("dve", o1[:, 0:n], o1[:, 0:n], rstda[:, 0:n], ALU.mult, ["o1", "rstda"], ["o1"])
                        ts("dve", yo[:, 0:n], o1[:, 0:n], subg[:, 0:1], ALU.mult, ["o1", "subg"], [kyo])
                        dma("pool", ycat_d[b, 2 + h, :, tok0:tok0 + n], yo[:, 0:n], r=[kyo])
                barrier()

            ar = Arena(arena_t, arena_kb * 256)
            w_out = ar.alloc([KC, D], BF16)
            for k in range(KC):
                dma("pool", w_out[:, k, :], w_out_d[l, k * 128:(k + 1) * 128, :], w=[("w_out", k)])
            dgs = [ar.alloc([128], F32) for _ in range(2)]
            g1rep = [ar.alloc([D], F32) for _ in range(2)]
            yct = [ar.alloc([8, 512], BF16) for _ in range(2)]
            xts = [ar.alloc([D], F32) for _ in range(3)]
            sq = ar.alloc([D], F32)
            ms = ar.alloc([2], F32)
            xn = ar.alloc([D], BF16)
            h2s = [ar.alloc([KC, 512], BF16) for _ in range(2)]
            for b in range(NB):
                gvec_rep(g1rep[0], 16, b, 0)
                if not last:
                    gvec_rep(g1rep[1], 16, NB, 0)
                xi = 0
                for si, (tok0, n, is_ctx, seg0, seglen) in enumerate(seg_tiles(b)):
                    if last and is_ctx:
                        continue
                    s = stream_of(b, is_ctx)
                    gr = g1rep[1 if is_ctx else 0]
                    yt = yct[si % 2]
                    kyt = ("yct", si % 2)
                    h2 = h2s[si % 2]
                    kh2 = ("h2s", si % 2)
                    dma("sp", yt[:, :, 0:n], ycat_d[b, :, :, tok0:tok0 + n].rearrange("c p t -> p c t"), w=[kyt])
                    for i in range(n // 128):
                        xt = xts[xi % 3]
                        kx = ("xt", xi % 3)
                        xi += 1
                        dma("sp", xt, x_src(l, b, tok0 + i * 128, 128), w=[kx])
                        for half in range(2):
                            for k in range(KC):
                                mm(bank(1 + half), yt[:, k, i * 128:(i + 1) * 128], w_out[:, k, half * 512:(half + 1) * 512],
                                   k == 0, k == KC - 1, [kyt, ("w_out", k)], [bk(1 + half)])
                        for half in range(2):
                            hs = slice(half * 512, (half + 1) * 512)
                            tt("dve", sq[:, hs], bank(1 + half), gr[:, hs], ALU.mult, [bk(1 + half), "grep"], ["sq"])
                            tt("dve", xt[:, hs], xt[:, hs], sq[:, hs], ALU.add, [kx, "sq"], [kx])
                        dma("pool", xs_d[b, tok0 + i * 128:tok0 + (i + 1) * 128, :], xt, r=[kx])
                        norm_modT(xt, sq, ms[:, 0:1], ms[:, 1:2], xn, h2[:, :, i * 128:(i + 1) * 128],
                                  A2[:, :, s], modT[:, 24:32, s], 3, kx, "ms", "xn", kh2)
                    dma("pool", h2T_d[b, :, :, tok0:tok0 + n].rearrange("c p t -> p c t"), h2[:, :, 0:n], r=[kh2])
            barrier()

            NJS = NJ // 2
            for sweep in range(2):
                ar = Arena(arena_t, arena_kb * 256)
                J0 = sweep * NJS
                wg = ar.alloc([KC, NJS * 128], BF16)
                wv = ar.alloc([KC, NJS * 128], BF16)
                wd = ar.alloc([NJS, D], BF16)
                for k in range(KC):
                    dma("pool", wg[:, k, :], w_gate_d[l, k * 128:(k + 1) * 128, J0 * 128:(J0 + NJS) * 128], w=[("wg", k)])
                    dma("pool", wv[:, k, :], w_val_d[l, k * 128:(k + 1) * 128, J0 * 128:(J0 + NJS) * 128], w=[("wv", k)])
                for j in range(NJS):
                    dma("pool", wd[:, j, :], w_down_d[l, (J0 + j) * 128:(J0 + j + 1) * 128, :], w=[("wd", j)])
                dgs = [ar.alloc([128], F32) for _ in range(2)]
                g2rep = [ar.alloc([D], F32) for _ in range(2)]
                h2in = [ar.alloc([KC, 514], BF16) for _ in range(2)]
                gts = [ar.alloc([514], F32) for _ in range(2)]
                acc = [ar.alloc([512], F32) for _ in range(2)]
                sg = [ar.alloc([512], F32) for _ in range(2)]
                aT = ar.alloc([NJS, 512], BF16)
                xts = [ar.alloc([D], F32) for _ in range(3)]
                tmp = ar.alloc([512], F32)
                for b in range(NB):
                    gvec_rep(g2rep[0], 40, b, 7)
                    if not last:
                        gvec_rep(g2rep[1], 40, NB, 7)
                    xi = 0
                    ji = 0
                    for si, (tok0, n, is_ctx, seg0, seglen) in enumerate(seg_tiles(b)):
                        if last and is_ctx:
                            continue
                        gr = g2rep[1 if is_ctx else 0]
                        hi_ = h2in[si % 2]
                        khi = ("h2in", si % 2)
                        lo = max(seg0, tok0 - 1)
                        hi = min(seg0 + seglen, tok0 + n + 1)
                        c0 = lo - (tok0 - 1)
                        if c0 > 0:
                            memset("dve", hi_[:, :, 0:c0], 0.0, [khi])
                        if hi < tok0 + n + 1:
                            memset("dve", hi_[:, :, n + 1:n + 2], 0.0, [khi])
                        dma("sp", hi_[:, :, c0:c0 + hi - lo], h2T_d[b, :, :, lo:hi].rearrange("c p t -> p c t"), w=[khi])
                        for j in range(NJS):
                            gb = ji % 2
                            g_ = gts[ji % 2]
                            a_ = acc[ji % 2]
                            s_ = sg[ji % 2]
                            kk_ = ji % 2
                            ji += 1
                            for k in range(KC):
                                mm(bank(gb)[:, 0:n], wg[:, k, j * 128:(j + 1) * 128], hi_[:, k, 1:n + 1],
                                   k == 0, k == KC - 1, [khi, ("wg", k)], [bk(gb)])
                            for k in range(KC):
                                mm(bank(4)[:, 0:2], wg[:, k, j * 128:(j + 1) * 128],
                                   hi_[:, k, 0:n + 2:n + 1], k == 0, k == KC - 1, [khi, ("wg", k)], [bk(4)])
                            for k in range(KC):
                                mm(bank(2 + gb)[:, 0:n], wv[:, k, j * 128:(j + 1) * 128], hi_[:, k, 1:n + 1],
                                   k == 0, k == KC - 1, [khi, ("wv", k)], [bk(2 + gb)])
                            cp("act", g_[:, 1:n + 1], bank(gb)[:, 0:n], [bk(gb)], [("gts", kk_)])
                            cp("act", g_[:, 0:n + 2:n + 1], bank(4)[:, 0:2], [bk(4)], [("gts", kk_)])
                            jj = J0 + j
                            ts("dve", a_[:, 0:n], g_[:, 0:n], fcw[:, jj, 0:1], ALU.mult, [("gts", kk_), "fcw"], [("acc", kk_)])
                            stt("dve", a_[:, 0:n], g_[:, 1:n + 1], fcw[:, jj, 1:2], a_[:, 0:n], ALU.mult, ALU.add,
                                [("gts", kk_), ("acc", kk_), "fcw"], [("acc", kk_)])
                            stt("dve", a_[:, 0:n], g_[:, 2:n + 2], fcw[:, jj, 2:3], a_[:, 0:n], ALU.mult, ALU.add,
                                [("gts", kk_), ("acc", kk_), "fcw"], [("acc", kk_)])
                            act(s_[:, 0:n], a_[:, 0:n], AF.Silu, [("acc", kk_), "fcb"], [("sg", kk_)], bias=fcb[:, jj:jj + 1])
                            tt("dve", aT[:, j, 0:n], s_[:, 0:n], bank(2 + gb)[:, 0:n], ALU.mult, [("sg", kk_), bk(2 + gb)], [("aT", j)])
                        for i in range(n // 128):
                            xt = xts[xi % 3]
                            kx = ("xt", xi % 3)
                            xi += 1
                            dma("sp", xt, xs_d[b, tok0 + i * 128:tok0 + (i + 1) * 128, :], w=[kx])
                            for half in range(2):
                                for j in range(NJS):
                                    mm(bank(5 + half), aT[:, j, i * 128:(i + 1) * 128], wd[:, j, half * 512:(half + 1) * 512],
                                       j == 0, j == NJS - 1, [("aT", j), ("wd", j)], [bk(5 + half)])
                            for half in range(2):
                                hs = slice(half * 512, (half + 1) * 512)
                                tt("dve", tmp, bank(5 + half), gr[:, hs], ALU.mult, [bk(5 + half), "grep"], ["tmp"])
                                tt("dve", xt[:, hs], xt[:, hs], tmp, ALU.add, [kx, "tmp"], [kx])
                            if last and sweep == 1:
                                dst = out_d[b, tok0 + i * 128:tok0 + (i + 1) * 128, :]
                            else:
                                dst = xs_d[b, tok0 + i * 128:tok0 + (i + 1) * 128, :]
                            dma("pool", dst, xt, r=[kx])
                barrier()

        S.emit(nc, st)
    return nc


def _strided2(ap2d, stride):
    return ap2d[:, 0:2 * stride].rearrange("p (a c) -> p a c", a=2)[:, :, 0]


def _rope_tables(L):
    ntl = L // 128
    tok = np.arange(L)
    row = (tok // GRID_W).astype(np.float32)
    col = (tok % GRID_W).astype(np.float32)
    inv = (10000.0 ** (-np.arange(0, 32, 2, dtype=np.float32) / 32)).astype(np.float32)
    ang = np.stack([row[:, None] * inv, col[:, None] * inv], axis=1).astype(np.float32)
    cos = np.cos(ang).astype(np.float32)
    sin = np.sin(ang).astype(np.float32)

    def lay(t):
        t = np.repeat(t[:, :, None, :], 2, axis=2).reshape(L, 64)
        return np.ascontiguousarray(t.reshape(ntl, 128, 64).transpose(1, 0, 2))
    return lay(cos), lay(sin)


def _fm(v, nchunk):
    sh = v.shape[:-1]
    return np.ascontiguousarray(np.swapaxes(v.reshape(*sh, nchunk, 128), -1, -2))


def make_in_maps(inputs, n_cores, NB, L, C):
    f = lambda a: np.ascontiguousarray(np.asarray(a, dtype=np.float32))
    x = f(inputs["x"]); c = f(inputs["c"]); ctx = f(inputs["ctx"]); c_ctx = f(inputs["c_ctx"])
    depth = inputs["w_mod"].shape[0]
    rcos, rsin = _rope_tables(L)
    shared = {
        "w_mod": f(inputs["w_mod"]),
        "b_modT": _fm(f(inputs["b_mod"]), 48),
        "norm1_gT": _fm(f(inputs["norm1_g"]), KC),
        "norm2_gT": _fm(f(inputs["norm2_g"]), KC),
        "w_in": f(inputs["w_in"]),
        "ln_v": np.ascontiguousarray(np.stack([f(inputs["ln_v_g"]), f(inputs["ln_v_b"])], axis=1)),
        "w_sT": np.ascontiguousarray(f(inputs["w_s"]).transpose(0, 1, 3, 2)),
        "b_sT": np.ascontiguousarray(f(inputs["b_s"]).transpose(0, 2, 1)),
        "qk_g": np.ascontiguousarray(np.stack([f(inputs["q_norm_g"]), f(inputs["k_norm_g"])], axis=1)),
        "lam": np.ascontiguousarray(np.stack([f(inputs["lam_q1"]), f(inputs["lam_k1"]),
                                              f(inputs["lam_q2"]), f(inputs["lam_k2"])], axis=1)),
        "subln_gT": np.ascontiguousarray(f(inputs["subln_g"])[:, :, None]),
        "conv_wT": np.ascontiguousarray(f(inputs["conv_w"]).reshape(depth, CONVK, 2, 128).transpose(0, 3, 2, 1)),
        "c_vecT": np.ascontiguousarray(np.stack([f(inputs["conv_b"]), f(inputs["ln_c_g"]), f(inputs["ln_c_b"])],
                                                axis=1).reshape(depth, 3, 2, 128).transpose(0, 3, 1, 2)),
        "w_out": f(inputs["w_out"]),
        "w_gate": f(inputs["w_gate"]),
        "w_val": f(inputs["w_val"]),
        "ffn_conv_wT": np.ascontiguousarray(f(inputs["ffn_conv_w"]).reshape(depth, 3, NJ, 128).transpose(0, 3, 2, 1)),
        "ffn_conv_bT": _fm(f(inputs["ffn_conv_b"]), NJ),
        "w_down": f(inputs["w_down"]),
        "rope_cos": rcos,
        "rope_sin": rsin,
    }
    maps = []
    for i in range(n_cores):
        bs = slice(i * NB, (i + 1) * NB)
        cv = np.concatenate([c[bs], c_ctx[None, :]], axis=0)
        cT = np.ascontiguousarray(cv.reshape(NB + 1, KC, 128).transpose(2, 1, 0))
        m = dict(shared)
        m["x"] = np.ascontiguousarray(x[bs])
        m["ctx"] = np.ascontiguousarray(ctx[bs])
        m["cT"] = cT
        maps.append(m)
    return maps


_NC_CACHE = {}


def kernel(**inputs):
    x = inputs["x"]
    B, L, _ = x.shape
    C = inputs["ctx"].shape[1]
    depth = inputs["w_mod"].shape[0]
    NB = B // N_CORES
    key = (L, C, NB, depth)
    if key not in _NC_CACHE:
        _NC_CACHE[key] = build_program(L, C, NB, depth)
    nc = _NC_CACHE[key]
    in_maps = make_in_maps(inputs, N_CORES, NB, L, C)
    res = run_bass_kernel_spmd(nc, in_maps, core_ids=list(range(N_CORES)))
    out = np.concatenate([np.asarray(r["out"], dtype=np.float32) for r in res.results], axis=0)
    return out.reshape(B, L, D)
```
